# Optimizing a Trainium2 kernel written in Bass

```python
import jax, jax.numpy as jnp
from jax import lax
import numpy as np

D_MODEL = 1024
BATCH = 16
SEQ = 2048
DEPTH = 4

GRID_W = 64
CTX_LEN = 256
NA_HEADS = 6
NA_HEAD_DIM = 64
NA_WIDTH = NA_HEADS * NA_HEAD_DIM
NA_WIN_H = 8
NA_WIN_W = 16
MLA_HEADS = 6
MLA_Q_RANK = 256
MLA_KV_RANK = 128
MLA_NOPE = 64
MLA_ROPE = 32
MLA_V = 64
LRU_WIDTH = 256
LRU_HEADS = 4
LRU_BLOCK = LRU_WIDTH // LRU_HEADS
LRU_CONV_W = 4
LRU_C = 8.0
MIX_WIDTH = NA_WIDTH + MLA_HEADS * MLA_V + LRU_WIDTH
FFN_HIDDEN = -(-(8 * D_MODEL) // (3 * 256)) * 256
IN_SPLITS = (NA_WIDTH, NA_WIDTH, NA_WIDTH, MLA_Q_RANK, MLA_KV_RANK, MLA_ROPE, LRU_WIDTH, LRU_WIDTH)
Q_BLOCK = 128
ROPE_BASE = 10000.0
EPS = 1e-6
NA_SCALE = NA_HEAD_DIM ** -0.5
MLA_SCALE = (MLA_NOPE + MLA_ROPE) ** -0.5

kernel_name = 'hybrid_na_mla_rglru_dit_block'


def _rmsnorm(x, g):
    xf = x.astype(jnp.float32)
    y = xf * lax.rsqrt(jnp.mean(xf * xf, axis=-1, keepdims=True) + EPS)
    return (y * g.astype(jnp.float32)).astype(x.dtype)


def _heads(t, n):
    return t.reshape(t.shape[:-1] + (n, t.shape[-1] // n))


def _adaln(cvec, w, b):
    m = jax.nn.silu(cvec) @ w + b
    return jnp.split(m[:, None, :], 6, axis=-1)


def _modulate(x, shift, scale):
    return x * (1.0 + scale) + shift


def _split_cols(u):
    return jnp.split(u, np.cumsum(IN_SPLITS)[:-1].tolist(), axis=-1)


def _axial_angles(n_tok, rot_dim):
    t = jnp.arange(n_tok)
    row = (t // GRID_W).astype(jnp.float32)
    col = (t % GRID_W).astype(jnp.float32)
    n_freq = rot_dim // 4
    inv = ROPE_BASE ** (-jnp.arange(n_freq, dtype=jnp.float32) / n_freq)
    return row[:, None] * inv, col[:, None] * inv


def _rope_half(z, ang):
    z1, z2 = jnp.split(z, 2, axis=-1)
    cos = jnp.cos(ang)[:, None, :]
    sin = jnp.sin(ang)[:, None, :]
    return jnp.concatenate([z1 * cos - z2 * sin, z1 * sin + z2 * cos], axis=-1)


def _axial_rope(z, angs):
    zr, zc = jnp.split(z, 2, axis=-1)
    return jnp.concatenate([_rope_half(zr, angs[0]), _rope_half(zc, angs[1])], axis=-1).astype(z.dtype)


def _attend(q, k, v, scale):
    s = jnp.einsum('bqhd,bkhd->bhqk', q, k, preferred_element_type=jnp.float32) * scale
    p = jax.nn.softmax(s, axis=-1).astype(v.dtype)
    return jnp.einsum('bhqk,bkhd->bqhd', p, v)


def _neighborhood_attention(q, k, v, k_ctx, v_ctx, rpb, rows):
    B, S, H, dh = q.shape
    kh = min(NA_WIN_H, rows)
    ncb = GRID_W // NA_WIN_W
    span = 2 * NA_WIN_W
    qcol = np.arange(GRID_W).reshape(ncb, NA_WIN_W)
    kcol = np.clip(qcol[:, :1] - NA_WIN_W // 2, 0, GRID_W - span) + np.arange(span)
    qcs = np.clip(qcol - NA_WIN_W // 2, 0, GRID_W - NA_WIN_W)[..., None]
    col_ok = (kcol[:, None, :] >= qcs) & (kcol[:, None, :] < qcs + NA_WIN_W)
    col_rel = np.clip(kcol[:, None, :] - qcol[..., None], 1 - NA_WIN_W, NA_WIN_W - 1) + NA_WIN_W - 1
    mask = np.broadcast_to(col_ok[:, :, None, :], (ncb, NA_WIN_W, kh, span)).reshape(ncb, NA_WIN_W, kh * span)
    qg = q.reshape(B, rows, GRID_W, H, dh)
    kg = k.reshape(B, rows, GRID_W, H, dh)
    vg = v.reshape(B, rows, GRID_W, H, dh)
    n_lat = kh * span

    def row_block(r):
        rs = jnp.clip(r - kh // 2, 0, rows - kh)
        k_rows = lax.dynamic_slice_in_dim(kg, rs, kh, axis=1)
        v_rows = lax.dynamic_slice_in_dim(vg, rs, kh, axis=1)
        k_blk = k_rows[:, :, kcol].transpose(0, 2, 1, 3, 4, 5).reshape(B, ncb, n_lat, H, dh)
        v_blk = v_rows[:, :, kcol].transpose(0, 2, 1, 3, 4, 5).reshape(B, ncb, n_lat, H, dh)
        q_blk = lax.dynamic_index_in_dim(qg, r, axis=1, keepdims=False).reshape(B, ncb, NA_WIN_W, H, dh)
        row_rel = rs + jnp.arange(kh) - r + NA_WIN_H - 1
        bias = rpb[:, row_rel][:, :, col_rel]
        bias = bias.transpose(0, 2, 3, 1, 4).reshape(H, ncb, NA_WIN_W, n_lat).astype(jnp.float32)
        s_lat = jnp.einsum('bnqhd,bnkhd->bhnqk', q_blk, k_blk, preferred_element_type=jnp.float32) * NA_SCALE + bias
        s_lat = jnp.where(mask, s_lat, -jnp.inf)
        s_ctx = jnp.einsum('bnqhd,bchd->bhnqc', q_blk, k_ctx, preferred_element_type=jnp.float32) * NA_SCALE
        p = jax.nn.softmax(jnp.concatenate([s_lat, s_ctx], axis=-1), axis=-1).astype(v.dtype)
        o = (jnp.einsum('bhnqk,bnkhd->bnqhd', p[..., :n_lat], v_blk)
             + jnp.einsum('bhnqc,bchd->bnqhd', p[..., n_lat:], v_ctx))
        return o.reshape(B, GRID_W, H, dh)

    out = lax.map(row_block, jnp.arange(rows))
    return out.transpose(1, 0, 2, 3, 4).reshape(B, S, H, dh)


def _mla_q(cq, cq_g, w_qb, q_g, angs):
    q = _heads(_rmsnorm(cq, cq_g) @ w_qb, MLA_HEADS)
    q_nope = _rmsnorm(q[..., :MLA_NOPE], q_g[:MLA_NOPE])
    q_rope = _rmsnorm(q[..., MLA_NOPE:], q_g[MLA_NOPE:])
    if angs is not None:
        q_rope = _axial_rope(q_rope, angs)
    return jnp.concatenate([q_nope, q_rope], axis=-1)


def _mla_kv(ckv, kr, ckv_g, w_kvb, k_g, angs):
    kv = _heads(_rmsnorm(ckv, ckv_g) @ w_kvb, MLA_HEADS)
    k_nope = _rmsnorm(kv[..., :MLA_NOPE], k_g[:MLA_NOPE])
    v = kv[..., MLA_NOPE:]
    k_rope = _rmsnorm(kr, k_g[MLA_NOPE:])[..., None, :]
    if angs is not None:
        k_rope = _axial_rope(k_rope, angs)
    k_rope = jnp.broadcast_to(k_rope, k_nope.shape[:-1] + (MLA_ROPE,))
    return jnp.concatenate([k_nope, k_rope], axis=-1), v


def _mla_latent_attention(q, k, v, k_ctx, v_ctx):
    B, S, H, dq = q.shape
    kk = jnp.concatenate([k_ctx, k], axis=1)
    vv = jnp.concatenate([v_ctx, v], axis=1)
    qb = q.reshape(B, S // Q_BLOCK, Q_BLOCK, H, dq).swapaxes(0, 1)
    out = lax.map(lambda qi: _attend(qi, kk, vv, MLA_SCALE), qb)
    return out.swapaxes(0, 1).reshape(B, S, H, v.shape[-1])


def _dwconv(x, w, b):
    y = lax.conv_general_dilated(x, w[:, None, :].astype(x.dtype), window_strides=(1,),
                                 padding=[((LRU_CONV_W - 1) // 2, LRU_CONV_W // 2)],
                                 dimension_numbers=('NWC', 'WIO', 'NWC'), feature_group_count=x.shape[-1])
    return y + b


def _rglru_coeffs(xc, w_a, b_a, w_x, b_x, lam):
    B, L, W = xc.shape
    xf = xc.astype(jnp.float32)
    xh = xf.reshape(B, L, LRU_HEADS, LRU_BLOCK)
    gate_a = jnp.einsum('blhi,dhij->dblhj', xh, w_a.astype(jnp.float32)).reshape(2, B, L, W) + b_a.astype(jnp.float32)[:, None, None, :]
    gate_x = jnp.einsum('blhi,dhij->dblhj', xh, w_x.astype(jnp.float32)).reshape(2, B, L, W) + b_x.astype(jnp.float32)[:, None, None, :]
    log_a = -LRU_C * jax.nn.sigmoid(gate_a) * jax.nn.softplus(-lam.astype(jnp.float32))[:, None, None, :]
    a = jnp.exp(log_a)
    u = jnp.sqrt(-jnp.expm1(2.0 * log_a)) * (jax.nn.sigmoid(gate_x) * xf[None])
    return a, u


def _linear_scan(a, u, h0, reverse):
    def step(h, au):
        h = au[0] * h + au[1]
        return h, h
    h_last, hs = lax.scan(step, h0, (a.swapaxes(0, 1), u.swapaxes(0, 1)), reverse=reverse)
    return hs.swapaxes(0, 1), h_last


def _swiglu(h, w_gate, w_up, w_down):
    return (jax.nn.silu(h @ w_gate) * (h @ w_up)) @ w_down


def setup_inputs(seed: int = 0) -> dict:
    key = jax.random.key(seed)
    ks = iter(jax.random.split(key, 32))
    f32 = jnp.float32

    def nrm(shape, scale):
        return jax.random.normal(next(ks), shape, f32) * scale

    def gain(shape):
        return 1.0 + nrm(shape, 0.05)

    lam_u = jax.random.uniform(next(ks), (DEPTH, 2, LRU_WIDTH), f32, 0.9, 0.999)
    a0 = lam_u ** (1.0 / LRU_C)
    return {
        'x': nrm((BATCH, SEQ, D_MODEL), 1.0),
        'c': nrm((BATCH, D_MODEL), 1.0),
        'ctx': nrm((BATCH, CTX_LEN, D_MODEL), 1.0),
        'c_ctx': nrm((D_MODEL,), 1.0),
        'ada_w': nrm((DEPTH, D_MODEL, 6 * D_MODEL), 0.5 * D_MODEL ** -0.5),
        'ada_b': nrm((DEPTH, 6 * D_MODEL), 0.02),
        'norm_mix_g': gain((DEPTH, D_MODEL)),
        'norm_ffn_g': gain((DEPTH, D_MODEL)),
        'w_in': nrm((DEPTH, D_MODEL, sum(IN_SPLITS)), D_MODEL ** -0.5),
        'na_q_g': gain((DEPTH, NA_HEAD_DIM)),
        'na_k_g': gain((DEPTH, NA_HEAD_DIM)),
        'na_rpb': nrm((DEPTH, NA_HEADS, 2 * NA_WIN_H - 1, 2 * NA_WIN_W - 1), 0.2),
        'mla_cq_g': gain((DEPTH, MLA_Q_RANK)),
        'mla_w_qb': nrm((DEPTH, MLA_Q_RANK, MLA_HEADS * (MLA_NOPE + MLA_ROPE)), MLA_Q_RANK ** -0.5),
        'mla_ckv_g': gain((DEPTH, MLA_KV_RANK)),
        'mla_w_kvb': nrm((DEPTH, MLA_KV_RANK, MLA_HEADS * (MLA_NOPE + MLA_V)), MLA_KV_RANK ** -0.5),
        'mla_q_g': gain((DEPTH, MLA_NOPE + MLA_ROPE)),
        'mla_k_g': gain((DEPTH, MLA_NOPE + MLA_ROPE)),
        'lru_conv_w': nrm((DEPTH, LRU_CONV_W, LRU_WIDTH), LRU_CONV_W ** -0.5),
        'lru_conv_b': nrm((DEPTH, LRU_WIDTH), 0.02),
        'lru_w_a': nrm((DEPTH, 2, LRU_HEADS, LRU_BLOCK, LRU_BLOCK), LRU_BLOCK ** -0.5),
        'lru_b_a': nrm((DEPTH, 2, LRU_WIDTH), 0.02),
        'lru_w_x': nrm((DEPTH, 2, LRU_HEADS, LRU_BLOCK, LRU_BLOCK), LRU_BLOCK ** -0.5),
        'lru_b_x': nrm((DEPTH, 2, LRU_WIDTH), 0.02),
        'lru_lambda': jnp.log(a0) - jnp.log1p(-a0),
        'w_out': nrm((DEPTH, MIX_WIDTH, D_MODEL), MIX_WIDTH ** -0.5),
        'ffn_w_gate': nrm((DEPTH, D_MODEL, FFN_HIDDEN), D_MODEL ** -0.5),
        'ffn_w_up': nrm((DEPTH, D_MODEL, FFN_HIDDEN), D_MODEL ** -0.5),
        'ffn_w_down': nrm((DEPTH, FFN_HIDDEN, D_MODEL), FFN_HIDDEN ** -0.5),
    }


def reference(x, c, ctx, c_ctx, ada_w, ada_b, norm_mix_g, norm_ffn_g, w_in, na_q_g, na_k_g, na_rpb,
              mla_cq_g, mla_w_qb, mla_ckv_g, mla_w_kvb, mla_q_g, mla_k_g,
              lru_conv_w, lru_conv_b, lru_w_a, lru_b_a, lru_w_x, lru_b_x, lru_lambda,
              w_out, ffn_w_gate, ffn_w_up, ffn_w_down):
    B, S, _ = x.shape
    C = ctx.shape[1]
    rows = S // GRID_W
    angs = _axial_angles(S, MLA_ROPE)
    h, hc = x, ctx
    for l in range(DEPTH):
        last = l == DEPTH - 1
        sh_m, sc_m, g_m, sh_f, sc_f, g_f = _adaln(c, ada_w[l], ada_b[l])
        csh_m, csc_m, cg_m, csh_f, csc_f, cg_f = _adaln(c_ctx[None], ada_w[l], ada_b[l])

        u = _modulate(_rmsnorm(h, norm_mix_g[l]), sh_m, sc_m) @ w_in[l]
        uc = _modulate(_rmsnorm(hc, norm_mix_g[l]), csh_m, csc_m) @ w_in[l]
        na_q, na_k, na_v, cq, ckv, kr, lx, lg = _split_cols(u)
        cna_q, cna_k, cna_v, ccq, cckv, ckr, clx, clg = _split_cols(uc)

        q = _rmsnorm(_heads(na_q, NA_HEADS), na_q_g[l])
        k = _rmsnorm(_heads(na_k, NA_HEADS), na_k_g[l])
        v = _heads(na_v, NA_HEADS)
        kc = _rmsnorm(_heads(cna_k, NA_HEADS), na_k_g[l])
        vc = _heads(cna_v, NA_HEADS)
        na_o = _neighborhood_attention(q, k, v, kc, vc, na_rpb[l], rows)

        mq = _mla_q(cq, mla_cq_g[l], mla_w_qb[l], mla_q_g[l], angs)
        mk, mv = _mla_kv(ckv, kr, mla_ckv_g[l], mla_w_kvb[l], mla_k_g[l], angs)
        mkc, mvc = _mla_kv(cckv, ckr, mla_ckv_g[l], mla_w_kvb[l], mla_k_g[l], None)
        mla_o = _mla_latent_attention(mq, mk, mv, mkc, mvc)

        xcc = _dwconv(clx, lru_conv_w[l], lru_conv_b[l])
        a_c, u_c = _rglru_coeffs(xcc, lru_w_a[l], lru_b_a[l], lru_w_x[l], lru_b_x[l], lru_lambda[l])
        zeros = jnp.zeros((B, LRU_WIDTH), jnp.float32)
        hcf, hcf_last = _linear_scan(a_c[0], u_c[0], zeros, False)
        hcb, hcb_last = _linear_scan(a_c[1], u_c[1], zeros, True)
        xcl = _dwconv(lx, lru_conv_w[l], lru_conv_b[l])
        a_l, u_l = _rglru_coeffs(xcl, lru_w_a[l], lru_b_a[l], lru_w_x[l], lru_b_x[l], lru_lambda[l])
        hf, _ = _linear_scan(a_l[0], u_l[0], hcf_last, False)
        hb, _ = _linear_scan(a_l[1], u_l[1], hcb_last, True)
        lru_o = jax.nn.gelu(lg) * (hf + hb).astype(lg.dtype)

        o_lat = jnp.concatenate([na_o.reshape(B, S, NA_WIDTH), mla_o.reshape(B, S, MLA_HEADS * MLA_V), lru_o], axis=-1)
        h = h + g_m * (o_lat @ w_out[l])
        h = h + g_f * _swiglu(_modulate(_rmsnorm(h, norm_ffn_g[l]), sh_f, sc_f), ffn_w_gate[l], ffn_w_up[l], ffn_w_down[l])

        if not last:
            qc = _rmsnorm(_heads(cna_q, NA_HEADS), na_q_g[l])
            na_c = _attend(qc, kc, vc, NA_SCALE).reshape(B, C, NA_WIDTH)
            mqc = _mla_q(ccq, mla_cq_g[l], mla_w_qb[l], mla_q_g[l], None)
            mla_c = _attend(mqc, mkc, mvc, MLA_SCALE).reshape(B, C, MLA_HEADS * MLA_V)
            lru_c = jax.nn.gelu(clg) * (hcf + hcb).astype(clg.dtype)
            o_ctx = jnp.concatenate([na_c, mla_c, lru_c], axis=-1)
            hc = hc + cg_m * (o_ctx @ w_out[l])
            hc = hc + cg_f * _swiglu(_modulate(_rmsnorm(hc, norm_ffn_g[l]), csh_f, csc_f), ffn_w_gate[l], ffn_w_up[l], ffn_w_down[l])
    return h
```

```python
import contextlib
import numpy as np
import concourse.bass as bass
import concourse.mybir as mybir
from concourse.bass_utils import run_bass_kernel_spmd

F32 = mybir.dt.float32
BF16 = mybir.dt.bfloat16
AF = mybir.ActivationFunctionType
ALU = mybir.AluOpType

D = 1024
TC = 256
TL = 2048
T = TC + TL
NLAYER = 4
NB = 2
FH = 2816
NEG = -30000.0
EPS = 1e-6
NA_SCALE = 64 ** -0.5
MLA_SCALE = 96 ** -0.5
SL = 94
EPOCH = 12000
NDMA = 24
SAME_ENGINE_SYNC = True
STORE_Q = 'pool'


class Tl:
    def __init__(self, h, key):
        self.h = h
        self.key = key

    def __getitem__(self, idx):
        return self.h[idx]


class Prog:
    def __init__(self, nc):
        self.nc = nc
        self.names = ['pe', 'dve', 'act', 'pool', 'sp']
        self.ops = {e: [] for e in self.names}
        self.cnt = {e: 0 for e in self.names}
        self.known = {e: {} for e in self.names}
        self.last_w = {}
        self.readers = {}
        self.dma_cnt = [0] * NDMA
        self.dma_rr = 0
        self.dma_rrs = {}
        self.semkeys = set()
        self.out_tokens = []
        self.marks = []

    def _tok(self, e):
        self.cnt[e] += 1
        n = self.cnt[e] - 1
        return (('c', e, n // EPOCH), n % EPOCH + 1)

    def _deps(self, e, reads, writes):
        deps = {}

        def add(tok):
            if tok is None:
                return
            k, v = tok
            if deps.get(k, 0) < v:
                deps[k] = v
        for r in reads:
            assert r.key in self.last_w, f"read of never-written buffer {r.key}"
            add(self.last_w.get(r.key))
        self._raw = dict(deps)
        for w in writes:
            add(self.last_w.get(w.key))
            for t in self.readers.get(w.key, ()):
                add(t)
        return deps

    def _emit_waits(self, e, deps):
        for k, v in deps.items():
            if k[0] == 'c' and k[1] == e:
                if e == 'pe' or not SAME_ENGINE_SYNC:
                    continue
                v = getattr(self, '_raw', {}).get(k, 0)
                if v == 0:
                    continue
            if self.known[e].get(k, 0) >= v:
                continue
            self.known[e][k] = v
            self.semkeys.add(k)
            self.ops[e].append(('wait', k, v))

    def _record(self, tok, reads, writes):
        for r in reads:
            self.readers.setdefault(r.key, []).append(tok)
        for w in writes:
            self.last_w[w.key] = tok
            self.readers[w.key] = []

    def op(self, e, fn, reads, writes):
        deps = self._deps(e, reads, writes)
        self._emit_waits(e, deps)
        tok = self._tok(e)
        self.semkeys.add(tok[0])
        self.ops[e].append(('op', fn, tok[0], 1))
        self._record(tok, reads, writes)
        return tok

    def dma(self, e, out_ap, in_ap, reads, writes, is_output=False):
        deps = self._deps(e, reads, writes)
        rr = self.dma_rrs.setdefault(e, [0])
        j = rr[0] + (0 if e == 'sp' else NDMA // 2)
        rr[0] = (rr[0] + 1) % (NDMA // 2)
        k = ('d', j)
        if self.dma_cnt[j] > 0:
            if deps.get(k, 0) < 16 * self.dma_cnt[j]:
                deps[k] = 16 * self.dma_cnt[j]
        self._emit_waits(e, deps)
        self.dma_cnt[j] += 1
        tok = (k, 16 * self.dma_cnt[j])
        self.semkeys.add(k)
        self.ops[e].append(('op', lambda eng, o=out_ap, i=in_ap: eng.dma_start(out=o, in_=i), k, 16))
        self._record(tok, reads, writes)
        return tok

    def store(self, out_ap, in_ap, reads, writes):
        return self.dma(STORE_Q, out_ap, in_ap, reads, writes)

    def mark(self, name):
        self.marks.append((name, {e: len(self.ops[e]) for e in self.names}))

    def barrier(self, name=None):
        self.mark(name)
        latest = {}
        for e in self.names:
            if self.cnt[e] > 0:
                n = self.cnt[e] - 1
                latest[('c', e, n // EPOCH)] = n % EPOCH + 1
        for j in range(NDMA):
            if self.dma_cnt[j] > 0:
                latest[('d', j)] = 16 * self.dma_cnt[j]
        for e in self.names:
            deps = {k: v for k, v in latest.items() if not (k[0] == 'c' and k[1] == e)}
            self._emit_waits(e, deps)

    def emit(self, es):
        nc = self.nc
        sems = {}
        for k in sorted(self.semkeys, key=str):
            sems[k] = es.enter_context(nc.semaphore("s_" + "_".join(str(x) for x in k)))
        block = es.enter_context(nc.Block())
        binder = {'pe': block.tensor, 'dve': block.vector, 'act': block.scalar, 'pool': block.gpsimd,
                  'sp': block.sync}
        for e in self.names:
            lst = self.ops[e]

            def body(eng, lst=lst):
                for item in lst:
                    if item[0] == 'wait':
                        eng.wait_ge(sems[item[1]], item[2])
                    else:
                        item[1](eng).then_inc(sems[item[2]], item[3])
            binder[e](body)

    def mm(self, out, lhsT, rhs, start, stop, reads, writes):
        return self.op('pe', lambda g: g.matmul(out, lhsT=lhsT, rhs=rhs, start=start, stop=stop), reads, writes)

    def act(self, out, in_, func, reads, writes, bias=None, scale=None):
        kw = {}
        if bias is not None:
            kw['bias'] = bias
        if scale is not None:
            kw['scale'] = scale
        return self.op('act', lambda g: g.activation(out=out, in_=in_, func=func, **kw), reads, writes)

    def tt(self, e, out, in0, in1, op, reads, writes):
        return self.op(e, lambda g: g.tensor_tensor(out=out, in0=in0, in1=in1, op=op), reads, writes)

    def stt(self, e, out, in0, scalar, in1, op0, op1, reads, writes):
        return self.op(e, lambda g: g.scalar_tensor_tensor(out=out, in0=in0, scalar=scalar, in1=in1, op0=op0,
                                                           op1=op1), reads, writes)

    def ts(self, e, out, in0, s1, s2, op0, op1, reads, writes):
        if s2 is None:
            return self.op(e, lambda g: g.tensor_scalar(out=out, in0=in0, scalar1=s1, scalar2=None, op0=op0),
                           reads, writes)
        return self.op(e, lambda g: g.tensor_scalar(out=out, in0=in0, scalar1=s1, scalar2=s2, op0=op0, op1=op1),
                       reads, writes)

    def copy(self, e, out, in_, reads, writes):
        if e == 'act':
            return self.op('act', lambda g: g.copy(out=out, in_=in_), reads, writes)
        return self.op(e, lambda g: g.tensor_copy(out=out, in_=in_), reads, writes)

    def recip(self, out, in_, reads, writes):
        return self.op('dve', lambda g: g.reciprocal(out=out, in_=in_), reads, writes)

    def memset(self, e, ap, val, writes):
        return self.op(e, lambda g: g.memset(ap, val), [], writes)


LAST = {}


class Own:
    def __init__(self, items):
        self.free = list(items)

    def get(self):
        assert self.free, "Own pool exhausted"
        return self.free.pop(0)

    def put(self, x):
        self.free.append(x)


class NormTask:
    _is_norm = True

    def __init__(self, g):
        self.g = g

    def __next__(self):
        return next(self.g)


def run_tasks(tasks, W):
    active = []
    it = iter(tasks)

    pending = []

    def refill():
        while len(active) < W:
            if not pending:
                try:
                    pending.append(next(it))
                except StopIteration:
                    return
            t = pending[0]
            if isinstance(t, tuple):
                if not t[0].get('ready'):
                    return
                t = t[1]
            pending.pop(0)
            active.append(t)
    refill()
    while active:
        g = active.pop(0)
        try:
            next(g)
            active.append(g)
        except StopIteration:
            pass
        refill()


class Pool:
    def __init__(self, tiles):
        self.tiles = tiles
        self.i = 0

    def next(self):
        t = self.tiles[self.i]
        self.i = (self.i + 1) % len(self.tiles)
        return t


def build_program(nlayer=NLAYER, debug=False):
    nc = bass.Bass("TRN2", target_bir_lowering=False)
    P = Prog(nc)
    LAST['P'] = P

    def din(name, shape, dt=F32):
        return nc.dram_tensor(name, list(shape), dt, kind="ExternalInput").ap()

    def dscr(name, shape, dt):
        kind = "ExternalOutput" if debug else "Internal"
        return nc.dram_tensor(name, list(shape), dt, kind=kind).ap()

    hT0 = din("hT0", [NB, D, T])
    csT = din("csT", [128, 24])
    ada_w = din("ada_w", [NLAYER, D, 6 * D])
    smalls_d = din("smalls", [128, NLAYER * SL])
    w_in_d = din("w_in", [NLAYER, D, 2080])
    w_qb_d = din("w_qb", [NLAYER, 256, 576])
    w_kvb_d = din("w_kvb", [NLAYER, 128, 768])
    wbd_d = din("wbd", [NLAYER, 128, 8 * 128])
    w_out_d = din("w_out", [NLAYER, D, D])
    wg_d = din("wg", [NLAYER, D, FH])
    wu_d = din("wu", [NLAYER, D, FH])
    wd_d = din("wd", [NLAYER, FH, D])
    cmat_d = din("cmat", [128, 7 * 128])
    rope_d = din("rope", [96, 2 * TL])
    natab_d = din("natab", [NLAYER, 6, 128, 21 * 128])
    outT = nc.dram_tensor("outT", [NB, D, TL], F32, kind="ExternalOutput").ap()

    H = dscr("H", [NB, D, T], F32)
    QNA = dscr("QNA", [NB, 384, T], BF16)
    KNA = dscr("KNA", [NB, 384, T], BF16)
    VNA = dscr("VNA", [NB, 18, 128, 768], BF16)
    QM = dscr("QM", [NB, 6, 96, T], BF16)
    KM = dscr("KM", [NB, 6, 96, T], BF16)
    VM = dscr("VM", [NB, 18, 128, 768], BF16)
    LX = dscr("LX", [NB, 256, T], F32)
    GL = dscr("GL", [NB, 256, T], F32)
    OT = dscr("OT", [NB, D, T], BF16)
    H2 = dscr("H2", [NB, D, T], F32)
    XN = dscr("XN", [NB, D, T], BF16)

    uid = [0]

    with contextlib.ExitStack() as es:
        def sb(shape, dt, es_=None, name=None):
            uid[0] += 1
            nm = f"{name or 't'}{uid[0]}"
            h = (es_ or es).enter_context(nc.sbuf_tensor(nm, list(shape), dt))
            return Tl(h, nm)

        def sbpool(n, shape, dt, es_=None, name=None):
            return Pool([sb(shape, dt, es_, name) for _ in range(n)])

        psum_h = es.enter_context(nc.psum_tensor("psum_all", [128, 4096], F32))
        banks = [Tl(psum_h, f"bank{i}") for i in range(8)]
        bank_i = [0]

        acc_i = [0]

        def bank():
            i = bank_i[0]
            bank_i[0] = (bank_i[0] + 1) % 6
            return banks[i], i * 512

        def accbank():
            i = 6 + acc_i[0]
            acc_i[0] = (acc_i[0] + 1) % 2
            return banks[i], i * 512

        def bank2():
            i = ((bank_i[0] + 1) // 2 * 2) % 6
            bank_i[0] = (i + 2) % 6
            return banks[i], banks[i + 1], i * 512

        cmat = sb([128, 7 * 128], BF16, name="cmat")
        smalls = sb([128, NLAYER * SL], F32, name="smalls")
        mod = sb([128, NLAYER * 48 * 3], F32, name="mod")
        gsm = sb([128, NLAYER * 24], F32, name="gsm")
        gsf = sb([128, NLAYER * 24], F32, name="gsf")
        negc = sb([128, NLAYER * 4], F32, name="negc")
        negc2 = sb([128, NLAYER * 4], F32, name="negc2")
        rope = sb([96, 2 * TL], F32, name="rope")
        sc = sb([128, 24], F32, name="sc")

        def cm(i, m=128):
            return cmat[0:m, i * 128:i * 128 + m]
        ONES1024, ONES256, ONES128, A64X2, A96, AKR, PROPE = range(7)

        def sm(l, col, rows=128):
            return smalls[0:rows, l * SL + col:l * SL + col + 1]

        def modv(l, j, k, col):
            o = (l * 48 + j * 8 + k) * 3 + col
            return mod[:, o:o + 1]

        def ada_load(l, hb, aw):
            P.dma('sp', aw[:, :].rearrange("p (k n) -> p k n", k=8),
                  ada_w[l, :, hb * 512:(hb + 1) * 512].rearrange("(k p) n -> p k n", p=128), [], [aw])

        def ada_mm(l, hb, aw):
            bk, bo = bank()
            for m in range(4):
                for k in range(8):
                    P.mm(psum_h[:, bo + m * 3:bo + m * 3 + 3],
                         aw[:, k * 512 + m * 128:k * 512 + (m + 1) * 128],
                         sc[:, k * 3:(k + 1) * 3], k == 0, k == 7, [aw, sc], [bk])
            for m in range(4):
                ch = hb * 4 + m
                o = (l * 48 + ch) * 3
                P.ts('dve', mod[:, o:o + 3], psum_h[:, bo + m * 3:bo + m * 3 + 3],
                     sm(l, ch), None, ALU.add, None, [bk, smalls], [mod])

        def ada_finish(l):
            for k in range(8):
                for col in range(3):
                    o = l * 24 + k * 3 + col
                    P.stt('dve', gsm[:, o:o + 1], modv(l, 1, k, col), sm(l, 48 + k), sm(l, 48 + k),
                          ALU.mult, ALU.add, [mod, smalls], [gsm])
                    P.stt('dve', gsf[:, o:o + 1], modv(l, 4, k, col), sm(l, 56 + k), sm(l, 56 + k),
                          ALU.mult, ALU.add, [mod, smalls], [gsf])

        P.dma('pool', cmat[:, :], cmat_d[:, :], [], [cmat])
        P.dma('sp', smalls[:, :], smalls_d[:, :], [], [smalls])
        P.dma('sp', rope[:, :], rope_d[:, :], [], [rope])
        P.dma('sp', sc[:, :], csT[:, :], [], [sc])
        for l in range(NLAYER):
            P.ts('dve', sm(l, 64), sm(l, 64), NA_SCALE, None, ALU.mult, None, [smalls], [smalls])
            P.ts('dve', sm(l, 69), sm(l, 69), MLA_SCALE, None, ALU.mult, None, [smalls], [smalls])
        with contextlib.ExitStack() as pes:
            tmpl = sb([128, 64], F32, pes)
            for l in range(nlayer):
                x_ = tmpl[:, 0:4]
                u_ = tmpl[:, 4:8]
                ln_ = tmpl[:, 8:12]
                um1 = tmpl[:, 12:16]
                P.act(x_, smalls[:, l * SL + 90:l * SL + 94], AF.Exp, [smalls], [tmpl], scale=-1.0)
                P.ts('dve', u_, x_, 1.0, None, ALU.add, None, [tmpl], [tmpl])
                P.act(ln_, u_, AF.Ln, [tmpl], [tmpl])
                P.ts('dve', um1, u_, -1.0, None, ALU.add, None, [tmpl], [tmpl])
                P.recip(um1, um1, [tmpl], [tmpl])
                P.tt('dve', ln_, ln_, x_, ALU.mult, [tmpl], [tmpl])
                P.tt('dve', ln_, ln_, um1, ALU.mult, [tmpl], [tmpl])
                P.ts('dve', negc[:, l * 4:l * 4 + 4], ln_, -8.0, None, ALU.mult, None, [tmpl], [negc])
                P.ts('dve', negc2[:, l * 4:l * 4 + 4], ln_, -16.0, None, ALU.mult, None, [tmpl], [negc2])
            P.act(sc[:, :], sc[:, :], AF.Silu, [sc], [sc])
            adap0 = sbpool(2, [128, 8 * 512], F32, pes, "ada")
            for hb in range(12):
                ada_load(0, hb, adap0.tiles[hb % 2])
                ada_mm(0, hb, adap0.tiles[hb % 2])
            ada_finish(0)
        P.barrier("P0")

        def chunks():
            for b in range(NB):
                yield b, 0, TC, True
                for c in range(4):
                    yield b, TC + c * 512, 512, False

        def rms_stats(pes_pools, ps_list, A_list, M, n, eps):
            sqp, rp = pes_pools
            bk2, bo2 = bank()
            for i, ((bk, ap), A) in enumerate(zip(ps_list, A_list)):
                sq = sqp.next()
                P.act(sq[0:M, 0:n], ap, AF.Square, [bk], [sq])
                P.mm(psum_h[0:M, bo2:bo2 + n], A, sq[0:M, 0:n], i == 0, i == len(ps_list) - 1, [sq, cmat], [bk2])
            r = rp.next()
            P.act(r[0:M, 0:n], psum_h[0:M, bo2:bo2 + n], AF.Ln, [bk2, epsb], [r], bias=epsb[0:M, 0:1])
            P.act(r[0:M, 0:n], r[0:M, 0:n], AF.Exp, [r], [r], scale=-0.5)
            return r

        epsb = sb([128, 1], F32, name="epsb")
        P.memset('dve', epsb[:, :], EPS, [epsb])

        for l in range(nlayer):
            last = (l == nlayer - 1)
            Hsrc = hT0 if l == 0 else H
            with contextlib.ExitStack() as pes:
                w_in = sb([128, 8 * 2080], BF16, pes, "w_in")
                w_qb = sb([128, 2 * 576], BF16, pes, "w_qb")
                w_kvb = sb([128, 768], BF16, pes, "w_kvb")
                wblk = [(1152, 1568), (0, 768), (768, 1152), (1568, 2080)]
                wkeys = [Tl(w_in.h, w_in.key + f"_b{i}") for i in range(4)]
                for (c_lo, c_hi), wk in zip(wblk, wkeys):
                    P.dma('pool', w_in[:, :].rearrange("p (k n) -> p k n", k=8)[:, :, c_lo:c_hi],
                          w_in_d[l][:, c_lo:c_hi].rearrange("(k p) n -> p k n", p=128), [], [wk])

                def wkey(c0):
                    for (c_lo, c_hi), wk in zip(wblk, wkeys):
                        if c_lo <= c0 < c_hi:
                            return wk
                    raise AssertionError(c0)
                P.dma('pool', w_qb[:, :].rearrange("p (k n) -> p k n", k=2),
                      w_qb_d[l].rearrange("(k p) n -> p k n", p=128), [], [w_qb])
                P.dma('pool', w_kvb[:, :], w_kvb_d[l], [], [w_kvb])
                hin_p = sbpool(2, [128, 8 * 512], F32, pes, "hin")
                sqb_p = sbpool(1, [128, 8 * 512], BF16, pes, "sqb")
                xn_p = sbpool(2, [128, 8 * 512], BF16, pes, "xn")
                tmp_p = sbpool(4, [128, 512], F32, pes, "tmp")
                sq_o = Own([sb([128, 512], BF16, pes, "sq") for _ in range(7)])
                r_p = sbpool(4, [128, 512], F32, pes, "r")
                rn_p = sbpool(2, [128, 512], F32, pes, "rn")
                ob_o = Own([sb([128, 512], BF16, pes, "ob") for _ in range(10)])
                of_p = sbpool(4, [128, 512], F32, pes, "of")
                cqn_p = sbpool(2, [128, 2 * 512], BF16, pes, "cqn")
                ckvn_p = sbpool(2, [128, 512], BF16, pes, "ckvn")
                vt_p = sbpool(4, [128, 768], BF16, pes, "vt")
                for vt in vt_p.tiles:
                    P.memset('pool', vt[:, :], 1.0, [vt])
                bo_ = Own(list(range(8)))

                def pbank():
                    i = bo_.get()
                    return i, banks[i], i * 512

                def t_norm(C):
                    b, t0, n, is_ctx, col = C['b'], C['t0'], C['n'], C['is_ctx'], C['col']
                    hin = hin_p.next()
                    P.dma('sp', hin[:, :].rearrange("p (k t) -> p k t", k=8)[:, :, 0:n],
                          Hsrc[b, :, t0:t0 + n].rearrange("(k p) t -> p k t", p=128), [], [hin])
                    sqb = sqb_p.next()
                    P.act(sqb[:, :].rearrange("p (k t) -> p k t", k=8)[:, :, 0:n],
                          hin[:, :].rearrange("p (k t) -> p k t", k=8)[:, :, 0:n], AF.Square, [hin], [sqb])
                    bi, bk, bo = pbank()
                    for k in range(8):
                        P.mm(psum_h[:, bo:bo + n], cm(ONES1024), sqb[:, k * 512:k * 512 + n], k == 0, k == 7,
                             [sqb, cmat], [bk])
                    yield
                    rstd = rn_p.next()
                    P.act(rstd[:, 0:n], psum_h[:, bo:bo + n], AF.Ln, [bk, epsb], [rstd], bias=epsb[:, 0:1])
                    bo_.put(bi)
                    P.act(rstd[:, 0:n], rstd[:, 0:n], AF.Exp, [rstd], [rstd], scale=-0.5)
                    xn = C['xn']
                    for k in range(8):
                        tmp = tmp_p.next()
                        o = l * 24 + k * 3 + col
                        P.stt('dve', tmp[:, 0:n], hin[:, k * 512:k * 512 + n], gsm[:, o:o + 1], rstd[:, 0:n],
                              ALU.mult, ALU.mult, [hin, gsm, rstd], [tmp])
                        if k % 2 == 1:
                            P.ts('dve', xn[:, k * 512:k * 512 + n], tmp[:, 0:n], modv(l, 0, k, col), None,
                                 ALU.add, None, [tmp, mod], [xn])
                        else:
                            P.act(xn[:, k * 512:k * 512 + n], tmp[:, 0:n], AF.Identity, [tmp, mod], [xn],
                                  bias=modv(l, 0, k, col))
                    C['ready'] = True

                def proj(C, c0, M):
                    n, xn = C['n'], C['xn']
                    bi, bk_, bo2 = pbank()
                    for k in range(8):
                        P.mm(psum_h[0:M, bo2:bo2 + n], w_in[:, k * 2080 + c0:k * 2080 + c0 + M],
                             xn[:, k * 512:k * 512 + n], k == 0, k == 7, [wkey(c0), xn], [bk_])
                    return bi, bk_, psum_h[0:M, bo2:bo2 + n]

                def square(bk_, pap, M, n):
                    sq = sq_o.get()
                    P.act(sq[0:M, 0:n], pap, AF.Square, [bk_], [sq])
                    return sq

                def stats(sqs, As, M, n):
                    bi, bk2, bo2 = pbank()
                    for i, (sq, A) in enumerate(zip(sqs, As)):
                        P.mm(psum_h[0:M, bo2:bo2 + n], A, sq[0:M, 0:n], i == 0, i == len(sqs) - 1, [sq, cmat], [bk2])
                        sq_o.put(sq)
                    r = r_p.next()
                    P.act(r[0:M, 0:n], psum_h[0:M, bo2:bo2 + n], AF.Ln, [bk2, epsb], [r], bias=epsb[0:M, 0:1])
                    bo_.put(bi)
                    P.act(r[0:M, 0:n], r[0:M, 0:n], AF.Exp, [r], [r], scale=-0.5)
                    return r

                def rope_mm(C, src, M):
                    n = C['n']
                    bi, bk_, bo2 = pbank()
                    P.mm(psum_h[0:M, bo2:bo2 + n], cm(PROPE, M), src[0:M, 0:n], True, True, [src, cmat], [bk_])
                    return bi, bk_, bo2

                def rope_fin(C, src, M, dst, bi, bk_, bo2):
                    n, t0 = C['n'], C['t0']
                    p0 = t0 - TC
                    t1 = tmp_p.next()
                    P.tt('dve', t1[0:M, 0:n], src[0:M, 0:n], rope[0:M, p0:p0 + n], ALU.mult, [src, rope], [t1])
                    t2 = tmp_p.next()
                    P.tt('dve', t2[0:M, 0:n], psum_h[0:M, bo2:bo2 + n], rope[0:M, TL + p0:TL + p0 + n],
                         ALU.mult, [bk_, rope], [t2])
                    bo_.put(bi)
                    P.tt('dve', dst[0:M, 0:n], t1[0:M, 0:n], t2[0:M, 0:n], ALU.add, [t1, t2], [dst])

                def t_na(C, which, j):
                    b, t0, n = C['b'], C['t0'], C['n']
                    cbase, gcol, dst = (0, 64, QNA) if which == "q" else (384, 65, KNA)
                    bi, bkp, pap = proj(C, cbase + j * 128, 128)
                    sq = square(bkp, pap, 128, n)
                    yield
                    r = stats([sq], [cm(A64X2)], 128, n)
                    ob = ob_o.get()
                    P.stt('dve', ob[:, 0:n], pap, sm(l, gcol), r[:, 0:n], ALU.mult, ALU.mult,
                          [bkp, smalls, r], [ob])
                    bo_.put(bi)
                    P.store(dst[b, j * 128:(j + 1) * 128, t0:t0 + n], ob[:, 0:n], [ob], [])
                    ob_o.put(ob)

                def t_v(C, tt_, src_kind):
                    b, t0, n, xn = C['b'], C['t0'], C['n'], C['xn']
                    bi, bk_, bo2 = pbank()
                    if src_kind == "na":
                        for k in range(8):
                            P.mm(psum_h[:, bo2:bo2 + 384], xn[:, k * 512 + tt_ * 128:k * 512 + (tt_ + 1) * 128],
                                 w_in[:, k * 2080 + 768:k * 2080 + 1152], k == 0, k == 7, [wkey(768), xn], [bk_])
                        dst = VNA
                    else:
                        ckvn = C['ckvn']
                        P.mm(psum_h[:, bo2:bo2 + 384], ckvn[:, tt_ * 128:(tt_ + 1) * 128], w_kvb[:, 384:768],
                             True, True, [w_kvb, ckvn], [bk_])
                        dst = VM
                    vt = vt_p.next()
                    P.copy('dve', vt[:, :].rearrange("p (h e) -> p h e", h=6)[:, :, 0:64],
                           psum_h[:, bo2:bo2 + 384].rearrange("p (h e) -> p h e", h=6), [bk_], [vt])
                    bo_.put(bi)
                    P.store(dst[b, (t0 // 128) + tt_], vt[:, :], [vt], [])
                    return
                    yield

                def t_cq(C):
                    n = C['n']
                    pl = [proj(C, 1152 + j * 128, 128) for j in range(2)]
                    sqs = [square(pl[j][1], pl[j][2], 128, n) for j in range(2)]
                    yield
                    r = stats(sqs, [cm(ONES256), cm(ONES256)], 128, n)
                    cqn = C['cqn']
                    for j in range(2):
                        P.stt('dve', cqn[:, j * 512:j * 512 + n], pl[j][2], sm(l, 66 + j), r[:, 0:n],
                              ALU.mult, ALU.mult, [pl[j][1], smalls, r], [cqn])
                        bo_.put(pl[j][0])

                def t_qhead(C, h):
                    b, t0, n, is_ctx, cqn = C['b'], C['t0'], C['n'], C['is_ctx'], C['cqn']
                    bi, bk_, bo2 = pbank()
                    for j in range(2):
                        P.mm(psum_h[0:96, bo2:bo2 + n], w_qb[:, j * 576 + h * 96:j * 576 + (h + 1) * 96],
                             cqn[:, j * 512:j * 512 + n], j == 0, j == 1, [w_qb, cqn], [bk_])
                    pap = psum_h[0:96, bo2:bo2 + n]
                    sq = square(bk_, pap, 96, n)
                    yield
                    r = stats([sq], [cm(A96, 96)], 96, n)
                    ob = ob_o.get()
                    P.stt('dve', ob[0:96, 0:n], pap, sm(l, 69, 96), r[0:96, 0:n], ALU.mult, ALU.mult,
                          [bk_, smalls, r], [ob])
                    bo_.put(bi)
                    if not is_ctx:
                        yield
                        rb = rope_mm(C, ob, 96)
                        yield
                        ob2 = ob_o.get()
                        rope_fin(C, ob, 96, ob2, *rb)
                        ob_o.put(ob)
                        ob = ob2
                    P.store(QM[b, h, :, t0:t0 + n], ob[0:96, 0:n], [ob], [])
                    ob_o.put(ob)

                def t_ckv(C):
                    n = C['n']
                    bi, bkp, pap = proj(C, 1408, 128)
                    sq = square(bkp, pap, 128, n)
                    yield
                    r = stats([sq], [cm(ONES128)], 128, n)
                    ckvn = C['ckvn']
                    P.stt('dve', ckvn[:, 0:n], pap, sm(l, 68), r[:, 0:n], ALU.mult, ALU.mult,
                          [bkp, smalls, r], [ckvn])
                    bo_.put(bi)

                def t_knope(C, j):
                    b, t0, n, ckvn = C['b'], C['t0'], C['n'], C['ckvn']
                    bi, bk_, bo2 = pbank()
                    P.mm(psum_h[:, bo2:bo2 + n], w_kvb[:, j * 128:(j + 1) * 128], ckvn[:, 0:n], True, True,
                         [w_kvb, ckvn], [bk_])
                    pap = psum_h[:, bo2:bo2 + n]
                    sq = square(bk_, pap, 128, n)
                    yield
                    r = stats([sq], [cm(A64X2)], 128, n)
                    ob = ob_o.get()
                    P.stt('dve', ob[:, 0:n], pap, sm(l, 70), r[:, 0:n], ALU.mult, ALU.mult,
                          [bk_, smalls, r], [ob])
                    bo_.put(bi)
                    P.store(KM[b, 2 * j, 0:64, t0:t0 + n], ob[0:64, 0:n], [ob], [])
                    P.store(KM[b, 2 * j + 1, 0:64, t0:t0 + n], ob[64:128, 0:n], [ob], [])
                    ob_o.put(ob)

                def t_kr(C):
                    b, t0, n, is_ctx = C['b'], C['t0'], C['n'], C['is_ctx']
                    bi, bkp, pap = proj(C, 1472, 96)
                    sq = square(bkp, pap, 96, n)
                    yield
                    r = stats([sq], [cm(AKR, 96)], 96, n)
                    ob = ob_o.get()
                    P.stt('dve', ob[0:96, 0:n], pap, sm(l, 71, 96), r[0:96, 0:n], ALU.mult, ALU.mult,
                          [bkp, smalls, r], [ob])
                    bo_.put(bi)
                    if not is_ctx:
                        yield
                        rb = rope_mm(C, ob, 96)
                        yield
                        ob2 = ob_o.get()
                        rope_fin(C, ob, 96, ob2, *rb)
                        ob_o.put(ob)
                        ob = ob2
                    for h in range(6):
                        P.store(KM[b, h, 64:96, t0:t0 + n], ob[64:96, 0:n], [ob], [])
                    ob_o.put(ob)

                def t_l(C, kind, j):
                    b, t0, n = C['b'], C['t0'], C['n']
                    c0 = (1568 if kind == "x" else 1824) + j * 128
                    bi, bkp, pap = proj(C, c0, 128)
                    of = of_p.next()
                    if kind == "x":
                        P.copy('act', of[:, 0:n], pap, [bkp], [of])
                        P.store(LX[b, j * 128:(j + 1) * 128, t0:t0 + n], of[:, 0:n], [of], [])
                    else:
                        P.act(of[:, 0:n], pap, AF.Gelu_apprx_tanh, [bkp], [of])
                        P.store(GL[b, j * 128:(j + 1) * 128, t0:t0 + n], of[:, 0:n], [of], [])
                    bo_.put(bi)
                    return
                    yield

                Cs = []
                for (b, t0, n, is_ctx) in chunks():
                    Cs.append(dict(b=b, t0=t0, n=n, is_ctx=is_ctx, col=(2 if is_ctx else b), xn=xn_p.next(),
                                   cqn=cqn_p.next(), ckvn=ckvn_p.next()))
                run_tasks([t_norm(Cs[0])], 1)
                tasks = []
                for ci, C in enumerate(Cs):
                    skipq = C['is_ctx'] and last
                    n = C['n']
                    tl = []
                    if not skipq:
                        tl.append(t_cq(C))
                    tl += [t_ckv(C), t_kr(C)]
                    if not skipq:
                        tl += [t_na(C, "q", j) for j in range(3)]
                    tl += [t_na(C, "k", j) for j in range(3)]
                    if ci + 1 < len(Cs):
                        tl.append(NormTask(t_norm(Cs[ci + 1])))
                    mix_a = [t_qhead(C, h) for h in range(6)] if not skipq else []
                    mix_b = [t_knope(C, j) for j in range(3)] + [t_v(C, tt_, "mla") for tt_ in range(n // 128)]
                    while mix_a or mix_b:
                        if mix_a:
                            tl.append(mix_a.pop(0))
                        if mix_b:
                            tl.append(mix_b.pop(0))
                    tl += [t_v(C, tt_, "na") for tt_ in range(n // 128)]
                    tl += [t_l(C, "x", j) for j in range(2)]
                    if not skipq:
                        tl += [t_l(C, "g", j) for j in range(2)]
                    tasks += [t if getattr(t, '_is_norm', False) else (C, t) for t in tl]
                run_tasks(tasks, 4)
            P.barrier("P1")

            with contextlib.ExitStack() as pes:
                qt_p = sbpool(2, [96, T], BF16, pes, "qt")
                kt_p = sbpool(2, [96, T], BF16, pes, "kt")
                v_p = sbpool(2, [128, 18 * 128], BF16, pes, "v")
                tab_p = sbpool(2, [128, 21 * 128], F32, pes, "tab")
                s_p = sbpool(3, [128, 5 * 128], F32, pes, "s")
                pna_p = sbpool(4, [128, 7 * 128], BF16, pes, "pna")
                pm_p = sbpool(4, [128, 1024], BF16, pes, "pm")
                u_p = sbpool(2, [128, T], F32, pes, "u")
                dn_p = sbpool(2, [64, T], F32, pes, "dn")
                o_p = sbpool(2, [64, T], BF16, pes, "o")

                def finish(u, q0, q1, orow, b, dn=None, o=None, eng='act', defer=None):
                    if dn is None:
                        dn = dn_p.next()
                        o = o_p.next()
                    P.store(dn[0:64, q0:q1], u[64:128, q0:q1], [u], [dn])

                    def part2():
                        if eng == 'act':
                            P.act(dn[0:64, q0:q1], dn[0:64, q0:q1], AF.Ln, [dn], [dn])
                            P.act(dn[0:64, q0:q1], dn[0:64, q0:q1], AF.Exp, [dn], [dn], scale=-1.0)
                        else:
                            P.recip(dn[0:64, q0:q1], dn[0:64, q0:q1], [dn], [dn])
                        P.tt('pool', o[0:64, q0:q1], u[0:64, q0:q1], dn[0:64, q0:q1], ALU.mult, [u, dn], [o])
                        P.store(OT[b, orow:orow + 64, q0:q1], o[0:64, q0:q1], [o], [])
                    if defer is None:
                        part2()
                    else:
                        defer.append(part2)

                fin_q = []

                do_ada = (l + 1 < nlayer)
                if do_ada:
                    adap = sbpool(2, [128, 8 * 512], F32, pes, "ada")
                for b in range(NB):
                    for h in range(6):
                        it = b * 6 + h
                        if do_ada:
                            ada_load(l + 1, it, adap.tiles[it % 2])
                            if it > 0:
                                ada_mm(l + 1, it - 1, adap.tiles[(it - 1) % 2])
                        qt = qt_p.next()
                        kt = kt_p.next()
                        v = v_p.next()
                        tab = tab_p.next()
                        P.dma('sp', qt[0:64, :], QNA[b, h * 64:(h + 1) * 64, :], [], [qt])
                        P.dma('sp', kt[0:64, :], KNA[b, h * 64:(h + 1) * 64, :], [], [kt])
                        P.dma('sp', v[:, :].rearrange("p (t e) -> p t e", t=18),
                              VNA[b, :, :, h * 128:(h + 1) * 128].rearrange("t p e -> p t e"), [], [v])
                        P.dma('sp', tab[:, :], natab_d[l, h], [], [tab])
                        u = u_p.next()
                        q_lo = 0 if not last else TC
                        if not last:
                            bks, bos = bank()
                            for kt_i in range(2):
                                P.mm(psum_h[:, bos + kt_i * 256:bos + (kt_i + 1) * 256],
                                     kt[0:64, kt_i * 128:(kt_i + 1) * 128], qt[0:64, 0:TC], True, True,
                                     [kt, qt], [bks])
                            p = pm_p.next()
                            P.act(p[:, 0:512], psum_h[:, bos:bos + 512], AF.Exp, [bks], [p])
                            bko, boo = accbank()
                            for kt_i in range(2):
                                P.mm(psum_h[:, boo:boo + TC], v[:, kt_i * 128:(kt_i + 1) * 128],
                                     p[:, kt_i * 256:(kt_i + 1) * 256], kt_i == 0, kt_i == 1, [v, p], [bko])
                            P.copy('dve', u[:, 0:TC], psum_h[:, boo:boo + TC], [bko], [u])

                        def na_S(i):
                            if i == 0:
                                kts, tb = [3, 2, 1, 0], 0
                            elif i == 1:
                                kts, tb = [3, 2, 1, 0], 4
                            elif i == 14:
                                kts, tb = [15, 14, 13, 12], 13
                            elif i == 15:
                                kts, tb = [15, 14, 13, 12], 17
                            else:
                                kts, tb = [i + 2, i + 1, i, i - 1, i - 2], 8
                            nl = len(kts)
                            q0 = TC + i * 128
                            bkA, bkB, so = bank2()
                            tiles = [2 + x for x in kts] + [0, 1]
                            for idx, kti in enumerate(tiles):
                                P.mm(psum_h[:, so + idx * 128:so + (idx + 1) * 128],
                                     kt[0:64, kti * 128:(kti + 1) * 128], qt[0:64, q0:q0 + 128], True, True,
                                     [kt, qt], [bkA if idx < 4 else bkB])
                            s_ = s_p.next()
                            P.tt('dve', s_[:, 0:nl * 128], psum_h[:, so:so + nl * 128],
                                 tab[:, tb * 128:(tb + nl) * 128], ALU.add, [bkA, bkB, tab], [s_])
                            p = pna_p.next()
                            P.act(p[:, 0:nl * 128], s_[:, 0:nl * 128], AF.Exp, [s_], [p])
                            P.act(p[:, nl * 128:(nl + 2) * 128], psum_h[:, so + nl * 128:so + (nl + 2) * 128],
                                  AF.Exp, [bkA, bkB], [p])
                            return i, p, tiles

                        acc = {}

                        def na_PV(i, p, tiles):
                            if i % 4 == 0:
                                acc['b'] = accbank()
                            bko, boo = acc['b']
                            for idx, kti in enumerate(tiles):
                                P.mm(psum_h[:, boo + (i % 4) * 128:boo + (i % 4 + 1) * 128],
                                     v[:, kti * 128:(kti + 1) * 128], p[:, idx * 128:(idx + 1) * 128],
                                     idx == 0, idx == len(tiles) - 1, [v, p], [bko])
                            if i % 4 == 3:
                                qs = TC + (i - 3) * 128
                                P.copy('dve', u[:, qs:qs + 512], psum_h[:, boo:boo + 512], [bko], [u])
                        pend = []
                        for i in range(16):
                            pend.append(na_S(i))
                            if len(pend) > 2:
                                na_PV(*pend.pop(0))
                            if i == 5:
                                while fin_q:
                                    fin_q.pop(0)()
                        while pend:
                            na_PV(*pend.pop(0))
                        finish(u, q_lo, T, h * 64, b, defer=fin_q)
                while fin_q:
                    fin_q.pop(0)()
                P.mark("NA")
                if do_ada:
                    ada_mm(l + 1, 11, adap.tiles[1])
                    ada_finish(l + 1)
                for b in range(NB):
                    for h in range(6):
                        qt = qt_p.next()
                        kt = kt_p.next()
                        v = v_p.next()
                        P.dma('sp', qt[0:96, :], QM[b, h], [], [qt])
                        P.dma('sp', kt[0:96, :], KM[b, h], [], [kt])
                        P.dma('sp', v[:, :].rearrange("p (t e) -> p t e", t=18),
                              VM[b, :, :, h * 128:(h + 1) * 128].rearrange("t p e -> p t e"), [], [v])
                        u = u_p.next()
                        qchunks = [(TC + c * 512, 512, 18) for c in range(4)]
                        if not last:
                            qchunks = [(0, TC, 2)] + qchunks
                        items = [(q0, qn, kti, nk) for (q0, qn, nk) in qchunks for kti in range(0, nk, 2)]
                        mdn = dn_p.next()
                        mo_ = o_p.next()

                        def m_S(item):
                            q0, qn, kti, nk = item
                            bkA, bkB, so = bank2()
                            for t_ in range(2):
                                P.mm(psum_h[:, so + t_ * 512:so + t_ * 512 + qn],
                                     kt[0:96, (kti + t_) * 128:(kti + t_ + 1) * 128],
                                     qt[0:96, q0:q0 + qn], True, True, [kt, qt], [bkA if t_ == 0 else bkB])
                            p = pm_p.next()
                            P.act(p[:, :].rearrange("p (t q) -> p t q", t=2)[:, :, 0:qn],
                                  psum_h[:, so:so + 1024].rearrange("p (t q) -> p t q", t=2)[:, :, 0:qn],
                                  AF.Exp, [bkA, bkB], [p])
                            return item, p

                        macc = {}

                        def m_PV(item, p):
                            q0, qn, kti, nk = item
                            if kti == 0:
                                macc['b'] = accbank()
                            bko, boo = macc['b']
                            for t_ in range(2):
                                P.mm(psum_h[:, boo:boo + qn], v[:, (kti + t_) * 128:(kti + t_ + 1) * 128],
                                     p[:, t_ * 512:t_ * 512 + qn],
                                     kti + t_ == 0, kti + t_ == nk - 1, [v, p], [bko])
                            if kti + 2 >= nk:
                                P.copy('dve', u[:, q0:q0 + qn], psum_h[:, boo:boo + qn], [bko], [u])
                                finish(u, q0, q0 + qn, 384 + h * 64, b, mdn, mo_, 'dve')
                        pend = []
                        for item in items:
                            pend.append(m_S(item))
                            if len(pend) > 2:
                                m_PV(*pend.pop(0))
                        while pend:
                            m_PV(*pend.pop(0))
            P.barrier("ATT")

            with contextlib.ExitStack() as pes:
                wbd = sb([128, 8 * 128], BF16, pes, "wbd")
                P.dma('pool', wbd[:, :], wbd_d[l], [], [wbd])
                sets = []
                for si in range(2):
                    sets.append(dict(
                        lx=sb([128, T], F32, pes, "lx"), xc=sb([128, T], F32, pes, "xc"),
                        xcb=sb([128, T], BF16, pes, "xcb"), a=sb([128, T], F32, pes, "a"),
                        sx=sb([128, T], F32, pes, "sx"), uu=sb([128, T], F32, pes, "lu"),
                        hf=sb([128, T], F32, pes, "hf"), hb=sb([128, T], F32, pes, "hb"),
                        ol=sb([128, T], BF16, pes, "ol")))
                segs = [(0, TC), (TC, T)]

                def t_lru(b, j, S):
                    lx, xc, xcb, a, sx, uu, hf, hb, ol = (S[k] for k in
                                                          ("lx", "xc", "xcb", "a", "sx", "uu", "hf", "hb", "ol"))
                    P.dma('sp', lx[:, :], LX[b, j * 128:(j + 1) * 128, :], [], [lx])
                    cw = lambda tap: sm(l, 72 + j * 4 + tap)
                    for (s0, s1) in segs:
                        P.ts('dve', xc[:, s0:s1], lx[:, s0:s1], cw(1), sm(l, 80 + j), ALU.mult, ALU.add,
                             [lx, smalls], [xc])
                        P.stt('dve', xc[:, s0 + 1:s1], lx[:, s0:s1 - 1], cw(0), xc[:, s0 + 1:s1],
                              ALU.mult, ALU.add, [lx, smalls, xc], [xc])
                        P.stt('dve', xc[:, s0:s1 - 1], lx[:, s0 + 1:s1], cw(2), xc[:, s0:s1 - 1],
                              ALU.mult, ALU.add, [lx, smalls, xc], [xc])
                        P.stt('dve', xc[:, s0:s1 - 2], lx[:, s0 + 2:s1], cw(3), xc[:, s0:s1 - 2],
                              ALU.mult, ALU.add, [lx, smalls, xc], [xc])
                    P.copy('act', xcb[:, :], xc[:, :], [xc], [xcb])
                    gl = lx
                    P.dma('sp', gl[:, :], GL[b, j * 128:(j + 1) * 128, :], [], [gl])
                    yield
                    for d in range(2):
                        for gate, dstt, bcol in ((0, a, 82), (1, sx, 86)):
                            wi = (gate * 2 + d) * 2 + j
                            for (c0, cn) in [(0, 512), (512, 512), (1024, 512), (1536, 512), (2048, 256)]:
                                bk_, bo_ = bank()
                                P.mm(psum_h[:, bo_:bo_ + cn], wbd[:, wi * 128:(wi + 1) * 128],
                                     xcb[:, c0:c0 + cn], True, True, [wbd, xcb], [bk_])
                                P.act(dstt[:, c0:c0 + cn], psum_h[:, bo_:bo_ + cn], AF.Sigmoid,
                                      [bk_, smalls], [dstt], bias=sm(l, bcol + d * 2 + j))
                        nci = l * 4 + d * 2 + j
                        P.act(uu[:, :], a[:, :], AF.Exp, [a, negc2], [uu], scale=negc2[:, nci:nci + 1])
                        P.act(a[:, :], a[:, :], AF.Exp, [a, negc], [a], scale=negc[:, nci:nci + 1])
                        P.act(uu[:, :], uu[:, :], AF.Sqrt, [uu], [uu], bias=1.0, scale=-1.0)
                        yield
                        P.tt('dve', uu[:, :], uu[:, :], sx[:, :], ALU.mult, [uu, sx], [uu])
                        P.tt('dve', uu[:, :], uu[:, :], xc[:, :], ALU.mult, [uu, xc], [uu])
                        if d == 0:
                            P.op('dve', lambda g, hf=hf, a=a, uu=uu: g.tensor_tensor_scan(
                                out=hf[:, :], data0=a[:, :], data1=uu[:, :], initial=0.0,
                                op0=ALU.mult, op1=ALU.add), [a, uu], [hf])
                        else:
                            P.op('dve', lambda g, hb=hb, a=a, uu=uu: g.tensor_tensor_scan(
                                out=hb[:, 0:TC], data0=a[:, 0:TC][:, ::-1], data1=uu[:, 0:TC][:, ::-1],
                                initial=0.0, op0=ALU.mult, op1=ALU.add), [a, uu], [hb])
                            P.op('dve', lambda g, hb=hb, a=a, uu=uu: g.tensor_tensor_scan(
                                out=hb[:, TC:T], data0=a[:, TC:T][:, ::-1], data1=uu[:, TC:T][:, ::-1],
                                initial=hb[:, TC - 1:TC], op0=ALU.mult, op1=ALU.add), [a, uu, hb], [hb])
                        yield
                    for (s0, s1) in segs:
                        P.tt('dve', hf[:, s0:s1], hf[:, s0:s1], hb[:, s0:s1][:, ::-1], ALU.add, [hf, hb], [hf])
                    P.tt('pool', ol[:, :], hf[:, :], gl[:, :], ALU.mult, [hf, gl], [ol])
                    q_lo = TC if last else 0
                    P.store(OT[b, 768 + j * 128:768 + (j + 1) * 128, q_lo:T], ol[:, q_lo:T], [ol], [])

                run_tasks([t_lru(b, j, sets[(b * 2 + j) % 2]) for b in range(NB) for j in range(2)], 2)
            P.barrier("LRU")

            with contextlib.ExitStack() as pes:
                wo = sb([128, 8 * D], BF16, pes, "wo")
                P.dma('pool', wo[:, :].rearrange("p (k n) -> p k n", k=8),
                      w_out_d[l].rearrange("(k p) n -> p k n", p=128), [], [wo])
                wgs = [sb([128, 8 * 256], BF16, pes, "wg") for _ in range(6)]
                wus = [sb([128, 8 * 256], BF16, pes, "wu") for _ in range(6)]
                wds = [sb([128, 2 * D], BF16, pes, "wd") for _ in range(6)]
                N2 = 512
                hin_p = sbpool(2, [128, 8 * N2], F32, pes, "h2")
                ot_p = sbpool(1, [128, 8 * N2], BF16, pes, "ot")
                sqb_p = sbpool(1, [128, 8 * N2], BF16, pes, "sqb2")
                xn_p = sbpool(2, [128, 8 * N2], BF16, pes, "xn2")
                r_p = sbpool(2, [128, N2], F32, pes, "r2")
                tmp_p = sbpool(3, [128, N2], F32, pes, "tmp2")
                sl_p = sbpool(3, [128, N2], F32, pes, "sl")
                g_p = sbpool(1, [128, 12 * N2], BF16, pes, "g")
                hkeys = {}

                def v3(t, n):
                    return t[:, :].rearrange("p (k t) -> p k t", k=8)[:, :, 0:n]

                def d3(dt, b, t0, n):
                    return dt[b, :, t0:t0 + n].rearrange("(k p) t -> p k t", p=128)
                for half in range(2):
                    sl_ids = list(range(6)) if half == 0 else list(range(6, 11))
                    for si, m2 in enumerate(sl_ids):
                        P.dma('pool', wgs[si][:, :].rearrange("p (k n) -> p k n", k=8),
                              wg_d[l, :, m2 * 256:(m2 + 1) * 256].rearrange("(k p) n -> p k n", p=128), [], [wgs[si]])
                        P.dma('pool', wus[si][:, :].rearrange("p (k n) -> p k n", k=8),
                              wu_d[l, :, m2 * 256:(m2 + 1) * 256].rearrange("(k p) n -> p k n", p=128), [], [wus[si]])
                    for si, m2 in enumerate(sl_ids):
                        P.dma('pool', wds[si][:, :].rearrange("p (k n) -> p k n", k=2),
                              wd_d[l, m2 * 256:(m2 + 1) * 256, :].rearrange("(k p) n -> p k n", p=128), [], [wds[si]])
                    nm = 2 * len(sl_ids)
                    def bank4():
                        i = b4[0]
                        b4[0] = (b4[0] + 1) % 4
                        return banks[i], i * 512

                    def t_ef(half, nm, ci, b, t0, n, is_ctx, hin, xn):
                        col = 2 if is_ctx else b
                        hk = hkeys.setdefault((b, t0), Tl(None, f"Hk{l}_{b}_{t0}"))
                        xk = hkeys.setdefault((b, t0, 'x'), Tl(None, f"Xk{l}_{b}_{t0}"))
                        if half == 0:
                            P.dma('sp', v3(hin, n), d3(Hsrc, b, t0, n), [], [hin])
                            ot = ot_p.next()
                            P.dma('sp', v3(ot, n), d3(OT, b, t0, n), [], [ot])
                            for m in range(8):
                                bk_, bo_ = bank4()
                                for k in range(8):
                                    P.mm(psum_h[:, bo_:bo_ + n], wo[:, k * D + m * 128:k * D + (m + 1) * 128],
                                         ot[:, k * N2:k * N2 + n], k == 0, k == 7, [wo, ot], [bk_])
                                P.stt('dve', hin[:, m * N2:m * N2 + n], psum_h[:, bo_:bo_ + n],
                                      modv(l, 2, m, col), hin[:, m * N2:m * N2 + n], ALU.mult, ALU.add,
                                      [bk_, mod, hin], [hin])
                            sqb = sqb_p.next()
                            P.act(v3(sqb, n), v3(hin, n), AF.Square, [hin], [sqb])
                            bk, bo = banks[4 + ci % 2], (4 + ci % 2) * 512
                            for k in range(8):
                                P.mm(psum_h[:, bo:bo + n], cm(ONES1024), sqb[:, k * N2:k * N2 + n], k == 0,
                                     k == 7, [sqb, cmat], [bk])
                            yield
                            rstd = r_p.next()
                            P.act(rstd[:, 0:n], psum_h[:, bo:bo + n], AF.Ln, [bk, epsb], [rstd],
                                  bias=epsb[:, 0:1])
                            P.act(rstd[:, 0:n], rstd[:, 0:n], AF.Exp, [rstd], [rstd], scale=-0.5)
                            for k in range(8):
                                tmp = tmp_p.next()
                                o = l * 24 + k * 3 + col
                                P.stt('dve', tmp[:, 0:n], hin[:, k * N2:k * N2 + n], gsf[:, o:o + 1],
                                      rstd[:, 0:n], ALU.mult, ALU.mult, [hin, gsf, rstd], [tmp])
                                P.act(xn[:, k * N2:k * N2 + n], tmp[:, 0:n], AF.Identity, [tmp, mod], [xn],
                                      bias=modv(l, 3, k, col))
                            P.store(d3(XN, b, t0, n), v3(xn, n), [xn], [xk])
                            yield
                        else:
                            P.dma('sp', v3(hin, n), d3(H2, b, t0, n), [hk], [hin])
                            P.dma('sp', v3(xn, n), d3(XN, b, t0, n), [xk], [xn])
                            yield
                        g = g_p.next()
                        for m in range(nm):
                            wgt, wut = wgs[m // 2], wus[m // 2]
                            bkg, bog = bank4()
                            bku, bou = bank4()
                            for k in range(8):
                                P.mm(psum_h[:, bog:bog + n],
                                     wgt[:, k * 256 + (m % 2) * 128:k * 256 + (m % 2 + 1) * 128],
                                     xn[:, k * N2:k * N2 + n], k == 0, k == 7, [wgt, xn], [bkg])
                            for k in range(8):
                                P.mm(psum_h[:, bou:bou + n],
                                     wut[:, k * 256 + (m % 2) * 128:k * 256 + (m % 2 + 1) * 128],
                                     xn[:, k * N2:k * N2 + n], k == 0, k == 7, [wut, xn], [bku])
                            sl = sl_p.next()
                            P.act(sl[:, 0:n], psum_h[:, bog:bog + n], AF.Silu, [bkg], [sl])
                            P.tt('dve', g[:, m * N2:m * N2 + n], sl[:, 0:n], psum_h[:, bou:bou + n], ALU.mult,
                                 [sl, bku], [g])
                        for mo in range(8):
                            bk_, bo_ = accbank()
                            for m in range(nm):
                                wdt = wds[m // 2]
                                P.mm(psum_h[:, bo_:bo_ + n],
                                     wdt[:, (m % 2) * D + mo * 128:(m % 2) * D + (mo + 1) * 128],
                                     g[:, m * N2:m * N2 + n], m == 0, m == nm - 1, [wdt, g], [bk_])
                            P.stt('dve', hin[:, mo * N2:mo * N2 + n], psum_h[:, bo_:bo_ + n],
                                  modv(l, 5, mo, col), hin[:, mo * N2:mo * N2 + n], ALU.mult, ALU.add,
                                  [bk_, mod, hin], [hin])
                        if half == 0:
                            P.store(d3(H2, b, t0, n), v3(hin, n), [hin], [hk])
                        elif last:
                            P.store(outT[b, :, t0 - TC:t0 - TC + n].rearrange("(k p) t -> p k t", p=128),
                                    v3(hin, n), [hin], [])
                        else:
                            P.store(d3(H, b, t0, n), v3(hin, n), [hin], [])

                    b4 = [0]
                    efl = []
                    ci = 0
                    for (b, t0, n, is_ctx) in chunks():
                        if is_ctx and last:
                            continue
                        efl.append(t_ef(half, nm, ci, b, t0, n, is_ctx, hin_p.tiles[ci % 2], xn_p.tiles[ci % 2]))
                        ci += 1
                    run_tasks(efl, 2)
            P.barrier("EF")
        P.emit(es)
    return nc


def _const_mats():
    m = np.zeros((7, 128, 128), np.float32)
    m[0] = 1.0 / 1024
    m[1] = 1.0 / 256
    m[2] = 1.0 / 128
    m[3, :64, :64] = 1.0 / 64
    m[3, 64:, 64:] = 1.0 / 64
    m[4, :64, :64] = 1.0 / 64
    m[4, 64:96, 64:96] = 1.0 / 32
    m[5, :64, :64] = 1.0 / 64
    m[5, 64:96, 64:96] = 1.0 / 32
    for base in (64, 80):
        for i in range(8):
            m[6, base + 8 + i, base + i] = -1.0
            m[6, base + i, base + 8 + i] = 1.0
    return np.ascontiguousarray(m.transpose(1, 0, 2).reshape(128, 7 * 128))


def _rope_tables():
    t = np.arange(TL)
    row = (t // 64).astype(np.float32)
    colp = (t % 64).astype(np.float32)
    inv = (np.float32(10000.0) ** (-np.arange(8, dtype=np.float32) / np.float32(8))).astype(np.float32)
    ar = row[:, None] * inv
    ac = colp[:, None] * inv
    cos = np.ones((96, TL), np.float32)
    sin = np.zeros((96, TL), np.float32)
    for base, ang in ((64, ar), (80, ac)):
        cos[base:base + 8] = np.cos(ang).T
        cos[base + 8:base + 16] = np.cos(ang).T
        sin[base:base + 8] = np.sin(ang).T
        sin[base + 8:base + 16] = np.sin(ang).T
    return np.ascontiguousarray(np.concatenate([cos, sin], axis=1))


def _na_index():
    idx = np.full((21, 128, 128), -1, np.int64)
    specs = [(0, [3, 2, 1, 0]), (1, [3, 2, 1, 0]), (5, [7, 6, 5, 4, 3]), (14, [15, 14, 13, 12]),
             (15, [15, 14, 13, 12])]
    kc = np.arange(64)[:, None]
    qc = np.arange(64)[None, :]
    qcs = np.clip(qc - 8, 0, 48)
    col_ok = (kc >= qcs) & (kc < qcs + 16)
    cr = np.clip(kc - qc, -15, 15) + 15
    ti = 0
    for i, kts in specs:
        for kt in kts:
            for a in range(2):
                for b in range(2):
                    kr = 2 * kt + a
                    qr = 2 * i + b
                    rs = min(max(qr - 4, 0), 24)
                    if rs <= kr < rs + 8:
                        dr = kr - qr + 7
                        blk = np.where(col_ok, dr * 31 + cr, -1)
                        idx[ti, a * 64:(a + 1) * 64, b * 64:(b + 1) * 64] = blk
            ti += 1
    return idx


def _prep_shared(inp):
    f = lambda k: np.asarray(inp[k], np.float32)
    sh = {}
    sh["ada_w"] = f("ada_w")
    sm = np.zeros((128, NLAYER, SL), np.float32)
    ada_b = f("ada_b")
    p = np.arange(128)
    for l in range(NLAYER):
        sm[:, l, 0:48] = ada_b[l].reshape(48, 128).T
        sm[:, l, 48:56] = f("norm_mix_g")[l].reshape(8, 128).T
        sm[:, l, 56:64] = f("norm_ffn_g")[l].reshape(8, 128).T
        sm[:, l, 64] = np.tile(f("na_q_g")[l], 2)
        sm[:, l, 65] = np.tile(f("na_k_g")[l], 2)
        sm[:, l, 66:68] = f("mla_cq_g")[l].reshape(2, 128).T
        sm[:, l, 68] = f("mla_ckv_g")[l]
        sm[:96, l, 69] = f("mla_q_g")[l]
        sm[:, l, 70] = np.tile(f("mla_k_g")[l][:64], 2)
        sm[64:96, l, 71] = f("mla_k_g")[l][64:96]
        cw = f("lru_conv_w")[l]
        for j in range(2):
            for tap in range(4):
                sm[:, l, 72 + j * 4 + tap] = cw[tap, j * 128:(j + 1) * 128]
            sm[:, l, 80 + j] = f("lru_conv_b")[l][j * 128:(j + 1) * 128]
            for d in range(2):
                sm[:, l, 82 + d * 2 + j] = f("lru_b_a")[l, d, j * 128:(j + 1) * 128]
                sm[:, l, 86 + d * 2 + j] = f("lru_b_x")[l, d, j * 128:(j + 1) * 128]
                sm[:, l, 90 + d * 2 + j] = f("lru_lambda")[l, d, j * 128:(j + 1) * 128]
    sh["smalls"] = np.ascontiguousarray(sm.reshape(128, NLAYER * SL))
    sh["w_in"] = f("w_in")
    sh["w_qb"] = f("mla_w_qb")
    kvb = f("mla_w_kvb").reshape(NLAYER, 128, 6, 128)
    sh["w_kvb"] = np.ascontiguousarray(
        np.concatenate([kvb[..., :64].reshape(NLAYER, 128, 384), kvb[..., 64:].reshape(NLAYER, 128, 384)], axis=-1))
    wbd = np.zeros((NLAYER, 128, 8, 128), np.float32)
    for gate, key in ((0, "lru_w_a"), (1, "lru_w_x")):
        w = f(key)
        for d in range(2):
            for j in range(2):
                wi = (gate * 2 + d) * 2 + j
                for hh in range(2):
                    wbd[:, hh * 64:(hh + 1) * 64, wi, hh * 64:(hh + 1) * 64] = w[:, d, j * 2 + hh]
    sh["wbd"] = np.ascontiguousarray(wbd.reshape(NLAYER, 128, 8 * 128))
    sh["w_out"] = f("w_out")
    sh["wg"] = f("ffn_w_gate")
    sh["wu"] = f("ffn_w_up")
    sh["wd"] = f("ffn_w_down")
    sh["cmat"] = _const_mats()
    sh["rope"] = _rope_tables()
    idx = _na_index()
    rpb = f("na_rpb").reshape(NLAYER, 6, 15 * 31)
    ext = np.concatenate([rpb, np.full((NLAYER, 6, 1), NEG, np.float32)], axis=-1)
    tab = ext[:, :, idx]
    sh["natab"] = np.ascontiguousarray(tab.transpose(0, 1, 3, 2, 4).reshape(NLAYER, 6, 128, 21 * 128))
    return sh


def _prep_core(inp, core):
    x = np.asarray(inp["x"], np.float32)
    ctx = np.asarray(inp["ctx"], np.float32)
    c = np.asarray(inp["c"], np.float32)
    c_ctx = np.asarray(inp["c_ctx"], np.float32)
    bs = slice(core * NB, (core + 1) * NB)
    hT0 = np.ascontiguousarray(np.concatenate([ctx[bs], x[bs]], axis=1).transpose(0, 2, 1))
    cols = np.stack([c[core * NB], c[core * NB + 1], c_ctx], axis=-1)
    csT = np.ascontiguousarray(cols.reshape(8, 128, 3).transpose(1, 0, 2).reshape(128, 24))
    return {"hT0": hT0, "csT": csT}


_NC_CACHE = {}


def kernel(**inputs):
    ncores = 8
    if "nc" not in _NC_CACHE:
        _NC_CACHE["nc"] = build_program()
    nc = _NC_CACHE["nc"]
    shared = _prep_shared(inputs)
    in_maps = []
    for core in range(ncores):
        m = dict(shared)
        m.update(_prep_core(inputs, core))
        in_maps.append(m)
    res = run_bass_kernel_spmd(nc, in_maps, core_ids=list(range(ncores)))
    outs = [np.asarray(r["outT"]) for r in res.results]
    out = np.concatenate(outs, axis=0).transpose(0, 2, 1)
    return np.ascontiguousarray(out.astype(np.float32))
```

```python
import contextlib
import numpy as np
import concourse.bass as bass
import concourse.mybir as mybir
from concourse.bass_utils import run_bass_kernel_spmd

F32 = mybir.dt.float32
BF16 = mybir.dt.bfloat16
AF = mybir.ActivationFunctionType
ALU = mybir.AluOpType

D = 1024
TC = 256
TL = 2048
T = TC + TL
NLAYER = 4
NB = 2
FH = 2816
NEG = -30000.0
EPS = 1e-6
NA_SCALE = 64 ** -0.5
MLA_SCALE = 96 ** -0.5
SL = 94
EPOCH = 12000
NDMA = 24
SAME_ENGINE_SYNC = True
STORE_Q = 'pool'


class Tl:
    def __init__(self, h, key):
        self.h = h
        self.key = key

    def __getitem__(self, idx):
        return self.h[idx]


class Prog:
    def __init__(self, nc):
        self.nc = nc
        self.names = ['pe', 'dve', 'act', 'pool', 'sp']
        self.ops = {e: [] for e in self.names}
        self.cnt = {e: 0 for e in self.names}
        self.known = {e: {} for e in self.names}
        self.last_w = {}
        self.readers = {}
        self.dma_cnt = [0] * NDMA
        self.dma_rr = 0
        self.dma_rrs = {}
        self.semkeys = set()
        self.out_tokens = []
        self.marks = []

    def _tok(self, e):
        self.cnt[e] += 1
        n = self.cnt[e] - 1
        return (('c', e, n // EPOCH), n % EPOCH + 1)

    def _deps(self, e, reads, writes):
        deps = {}

        def add(tok):
            if tok is None:
                return
            k, v = tok
            if deps.get(k, 0) < v:
                deps[k] = v
        for r in reads:
            assert r.key in self.last_w, f"read of never-written buffer {r.key}"
            add(self.last_w.get(r.key))
        self._raw = dict(deps)
        for w in writes:
            add(self.last_w.get(w.key))
            for t in self.readers.get(w.key, ()):
                add(t)
        return deps

    def _emit_waits(self, e, deps):
        for k, v in deps.items():
            if k[0] == 'c' and k[1] == e:
                if e == 'pe' or not SAME_ENGINE_SYNC:
                    continue
                v = getattr(self, '_raw', {}).get(k, 0)
                if v == 0:
                    continue
            if self.known[e].get(k, 0) >= v:
                continue
            self.known[e][k] = v
            self.semkeys.add(k)
            self.ops[e].append(('wait', k, v))

    def _record(self, tok, reads, writes):
        for r in reads:
            self.readers.setdefault(r.key, []).append(tok)
        for w in writes:
            self.last_w[w.key] = tok
            self.readers[w.key] = []

    def op(self, e, fn, reads, writes):
        deps = self._deps(e, reads, writes)
        self._emit_waits(e, deps)
        tok = self._tok(e)
        self.semkeys.add(tok[0])
        self.ops[e].append(('op', fn, tok[0], 1))
        self._record(tok, reads, writes)
        return tok

    def dma(self, e, out_ap, in_ap, reads, writes, is_output=False):
        deps = self._deps(e, reads, writes)
        rr = self.dma_rrs.setdefault(e, [0])
        j = rr[0] + (0 if e == 'sp' else NDMA // 2)
        rr[0] = (rr[0] + 1) % (NDMA // 2)
        k = ('d', j)
        if self.dma_cnt[j] > 0:
            if deps.get(k, 0) < 16 * self.dma_cnt[j]:
                deps[k] = 16 * self.dma_cnt[j]
        self._emit_waits(e, deps)
        self.dma_cnt[j] += 1
        tok = (k, 16 * self.dma_cnt[j])
        self.semkeys.add(k)
        self.ops[e].append(('op', lambda eng, o=out_ap, i=in_ap: eng.dma_start(out=o, in_=i), k, 16))
        self._record(tok, reads, writes)
        return tok

    def store(self, out_ap, in_ap, reads, writes):
        return self.dma(STORE_Q, out_ap, in_ap, reads, writes)

    def mark(self, name):
        self.marks.append((name, {e: len(self.ops[e]) for e in self.names}))

    def barrier(self, name=None):
        self.mark(name)
        latest = {}
        for e in self.names:
            if self.cnt[e] > 0:
                n = self.cnt[e] - 1
                latest[('c', e, n // EPOCH)] = n % EPOCH + 1
        for j in range(NDMA):
            if self.dma_cnt[j] > 0:
                latest[('d', j)] = 16 * self.dma_cnt[j]
        for e in self.names:
            deps = {k: v for k, v in latest.items() if not (k[0] == 'c' and k[1] == e)}
            self._emit_waits(e, deps)

    def emit(self, es):
        nc = self.nc
        sems = {}
        for k in sorted(self.semkeys, key=str):
            sems[k] = es.enter_context(nc.semaphore("s_" + "_".join(str(x) for x in k)))
        block = es.enter_context(nc.Block())
        binder = {'pe': block.tensor, 'dve': block.vector, 'act': block.scalar, 'pool': block.gpsimd,
                  'sp': block.sync}
        for e in self.names:
            lst = self.ops[e]

            def body(eng, lst=lst):
                for item in lst:
                    if item[0] == 'wait':
                        eng.wait_ge(sems[item[1]], item[2])
                    else:
                        item[1](eng).then_inc(sems[item[2]], item[3])
            binder[e](body)

    def mm(self, out, lhsT, rhs, start, stop, reads, writes):
        return self.op('pe', lambda g: g.matmul(out, lhsT=lhsT, rhs=rhs, start=start, stop=stop), reads, writes)

    def act(self, out, in_, func, reads, writes, bias=None, scale=None):
        kw = {}
        if bias is not None:
            kw['bias'] = bias
        if scale is not None:
            kw['scale'] = scale
        return self.op('act', lambda g: g.activation(out=out, in_=in_, func=func, **kw), reads, writes)

    def tt(self, e, out, in0, in1, op, reads, writes):
        return self.op(e, lambda g: g.tensor_tensor(out=out, in0=in0, in1=in1, op=op), reads, writes)

    def stt(self, e, out, in0, scalar, in1, op0, op1, reads, writes):
        return self.op(e, lambda g: g.scalar_tensor_tensor(out=out, in0=in0, scalar=scalar, in1=in1, op0=op0,
                                                           op1=op1), reads, writes)

    def ts(self, e, out, in0, s1, s2, op0, op1, reads, writes):
        if s2 is None:
            return self.op(e, lambda g: g.tensor_scalar(out=out, in0=in0, scalar1=s1, scalar2=None, op0=op0),
                           reads, writes)
        return self.op(e, lambda g: g.tensor_scalar(out=out, in0=in0, scalar1=s1, scalar2=s2, op0=op0, op1=op1),
                       reads, writes)

    def copy(self, e, out, in_, reads, writes):
        if e == 'act':
            return self.op('act', lambda g: g.copy(out=out, in_=in_), reads, writes)
        return self.op(e, lambda g: g.tensor_copy(out=out, in_=in_), reads, writes)

    def recip(self, out, in_, reads, writes):
        return self.op('dve', lambda g: g.reciprocal(out=out, in_=in_), reads, writes)

    def memset(self, e, ap, val, writes):
        return self.op(e, lambda g: g.memset(ap, val), [], writes)


LAST = {}


class Own:
    def __init__(self, items):
        self.free = list(items)

    def get(self):
        assert self.free, "Own pool exhausted"
        return self.free.pop(0)

    def put(self, x):
        self.free.append(x)


class NormTask:
    _is_norm = True

    def __init__(self, g):
        self.g = g

    def __next__(self):
        return next(self.g)


def run_tasks(tasks, W):
    active = []
    it = iter(tasks)

    pending = []

    def refill():
        while len(active) < W:
            if not pending:
                try:
                    pending.append(next(it))
                except StopIteration:
                    return
            t = pending[0]
            if isinstance(t, tuple):
                if not t[0].get('ready'):
                    return
                t = t[1]
            pending.pop(0)
            active.append(t)
    refill()
    while active:
        g = active.pop(0)
        try:
            next(g)
            active.append(g)
        except StopIteration:
            pass
        refill()


class Pool:
    def __init__(self, tiles):
        self.tiles = tiles
        self.i = 0

    def next(self):
        t = self.tiles[self.i]
        self.i = (self.i + 1) % len(self.tiles)
        return t


def build_program(nlayer=NLAYER, debug=False):
    nc = bass.Bass("TRN2", target_bir_lowering=False)
    P = Prog(nc)
    LAST['P'] = P

    def din(name, shape, dt=F32):
        return nc.dram_tensor(name, list(shape), dt, kind="ExternalInput").ap()

    def dscr(name, shape, dt):
        kind = "ExternalOutput" if debug else "Internal"
        return nc.dram_tensor(name, list(shape), dt, kind=kind).ap()

    hT0 = din("hT0", [NB, D, T])
    csT = din("csT", [128, 24])
    ada_w = din("ada_w", [NLAYER, D, 6 * D])
    smalls_d = din("smalls", [128, NLAYER * SL])
    w_in_d = din("w_in", [NLAYER, D, 2080])
    w_qb_d = din("w_qb", [NLAYER, 256, 576])
    w_kvb_d = din("w_kvb", [NLAYER, 128, 768])
    wbd_d = din("wbd", [NLAYER, 128, 8 * 128])
    w_out_d = din("w_out", [NLAYER, D, D])
    wg_d = din("wg", [NLAYER, D, FH])
    wu_d = din("wu", [NLAYER, D, FH])
    wd_d = din("wd", [NLAYER, FH, D])
    cmat_d = din("cmat", [128, 7 * 128])
    rope_d = din("rope", [96, 2 * TL])
    natab_d = din("natab", [NLAYER, 6, 128, 21 * 128])
    outT = nc.dram_tensor("outT", [NB, D, TL], F32, kind="ExternalOutput").ap()

    H = dscr("H", [NB, D, T], F32)
    QNA = dscr("QNA", [NB, 384, T], BF16)
    KNA = dscr("KNA", [NB, 384, T], BF16)
    VNA = dscr("VNA", [NB, 18, 128, 768], BF16)
    QM = dscr("QM", [NB, 6, 96, T], BF16)
    KM = dscr("KM", [NB, 6, 96, T], BF16)
    VM = dscr("VM", [NB, 18, 128, 768], BF16)
    LX = dscr("LX", [NB, 256, T], F32)
    GL = dscr("GL", [NB, 256, T], F32)
    OT = dscr("OT", [NB, D, T], BF16)
    H2 = dscr("H2", [NB, D, T], F32)
    XN = dscr("XN", [NB, D, T], BF16)

    uid = [0]

    with contextlib.ExitStack() as es:
        def sb(shape, dt, es_=None, name=None):
            uid[0] += 1
            nm = f"{name or 't'}{uid[0]}"
            h = (es_ or es).enter_context(nc.sbuf_tensor(nm, list(shape), dt))
            return Tl(h, nm)

        def sbpool(n, shape, dt, es_=None, name=None):
            return Pool([sb(shape, dt, es_, name) for _ in range(n)])

        psum_h = es.enter_context(nc.psum_tensor("psum_all", [128, 4096], F32))
        banks = [Tl(psum_h, f"bank{i}") for i in range(8)]
        bank_i = [0]

        acc_i = [0]

        def bank():
            i = bank_i[0]
            bank_i[0] = (bank_i[0] + 1) % 6
            return banks[i], i * 512

        def accbank():
            i = 6 + acc_i[0]
            acc_i[0] = (acc_i[0] + 1) % 2
            return banks[i], i * 512

        def bank2():
            i = ((bank_i[0] + 1) // 2 * 2) % 6
            bank_i[0] = (i + 2) % 6
            return banks[i], banks[i + 1], i * 512

        cmat = sb([128, 7 * 128], BF16, name="cmat")
        smalls = sb([128, NLAYER * SL], F32, name="smalls")
        mod = sb([128, NLAYER * 48 * 3], F32, name="mod")
        gsm = sb([128, NLAYER * 24], F32, name="gsm")
        gsf = sb([128, NLAYER * 24], F32, name="gsf")
        negc = sb([128, NLAYER * 4], F32, name="negc")
        negc2 = sb([128, NLAYER * 4], F32, name="negc2")
        rope = sb([96, 2 * TL], F32, name="rope")
        sc = sb([128, 24], F32, name="sc")

        def cm(i, m=128):
            return cmat[0:m, i * 128:i * 128 + m]
        ONES1024, ONES256, ONES128, A64X2, A96, AKR, PROPE = range(7)

        def sm(l, col, rows=128):
            return smalls[0:rows, l * SL + col:l * SL + col + 1]

        def modv(l, j, k, col):
            o = (l * 48 + j * 8 + k) * 3 + col
            return mod[:, o:o + 1]

        def ada_load(l, hb, aw):
            P.dma('sp', aw[:, :].rearrange("p (k n) -> p k n", k=8),
                  ada_w[l, :, hb * 512:(hb + 1) * 512].rearrange("(k p) n -> p k n", p=128), [], [aw])

        def ada_mm(l, hb, aw):
            bk, bo = bank()
            for m in range(4):
                for k in range(8):
                    P.mm(psum_h[:, bo + m * 3:bo + m * 3 + 3],
                         aw[:, k * 512 + m * 128:k * 512 + (m + 1) * 128],
                         sc[:, k * 3:(k + 1) * 3], k == 0, k == 7, [aw, sc], [bk])
            for m in range(4):
                ch = hb * 4 + m
                o = (l * 48 + ch) * 3
                P.ts('dve', mod[:, o:o + 3], psum_h[:, bo + m * 3:bo + m * 3 + 3],
                     sm(l, ch), None, ALU.add, None, [bk, smalls], [mod])

        def ada_finish(l):
            for k in range(8):
                for col in range(3):
                    o = l * 24 + k * 3 + col
                    P.stt('dve', gsm[:, o:o + 1], modv(l, 1, k, col), sm(l, 48 + k), sm(l, 48 + k),
                          ALU.mult, ALU.add, [mod, smalls], [gsm])
                    P.stt('dve', gsf[:, o:o + 1], modv(l, 4, k, col), sm(l, 56 + k), sm(l, 56 + k),
                          ALU.mult, ALU.add, [mod, smalls], [gsf])

        P.dma('pool', cmat[:, :], cmat_d[:, :], [], [cmat])
        P.dma('sp', smalls[:, :], smalls_d[:, :], [], [smalls])
        P.dma('sp', rope[:, :], rope_d[:, :], [], [rope])
        P.dma('sp', sc[:, :], csT[:, :], [], [sc])
        for l in range(NLAYER):
            P.ts('dve', sm(l, 64), sm(l, 64), NA_SCALE, None, ALU.mult, None, [smalls], [smalls])
            P.ts('dve', sm(l, 69), sm(l, 69), MLA_SCALE, None, ALU.mult, None, [smalls], [smalls])
        with contextlib.ExitStack() as pes:
            tmpl = sb([128, 64], F32, pes)
            for l in range(nlayer):
                x_ = tmpl[:, 0:4]
                u_ = tmpl[:, 4:8]
                ln_ = tmpl[:, 8:12]
                um1 = tmpl[:, 12:16]
                P.act(x_, smalls[:, l * SL + 90:l * SL + 94], AF.Exp, [smalls], [tmpl], scale=-1.0)
                P.ts('dve', u_, x_, 1.0, None, ALU.add, None, [tmpl], [tmpl])
                P.act(ln_, u_, AF.Ln, [tmpl], [tmpl])
                P.ts('dve', um1, u_, -1.0, None, ALU.add, None, [tmpl], [tmpl])
                P.recip(um1, um1, [tmpl], [tmpl])
                P.tt('dve', ln_, ln_, x_, ALU.mult, [tmpl], [tmpl])
                P.tt('dve', ln_, ln_, um1, ALU.mult, [tmpl], [tmpl])
                P.ts('dve', negc[:, l * 4:l * 4 + 4], ln_, -8.0, None, ALU.mult, None, [tmpl], [negc])
                P.ts('dve', negc2[:, l * 4:l * 4 + 4], ln_, -16.0, None, ALU.mult, None, [tmpl], [negc2])
            P.act(sc[:, :], sc[:, :], AF.Silu, [sc], [sc])
            adap0 = sbpool(2, [128, 8 * 512], F32, pes, "ada")
            for hb in range(12):
                ada_load(0, hb, adap0.tiles[hb % 2])
                ada_mm(0, hb, adap0.tiles[hb % 2])
            ada_finish(0)
        P.barrier("P0")

        def chunks():
            for b in range(NB):
                yield b, 0, TC, True
                for c in range(4):
                    yield b, TC + c * 512, 512, False

        def rms_stats(pes_pools, ps_list, A_list, M, n, eps):
            sqp, rp = pes_pools
            bk2, bo2 = bank()
            for i, ((bk, ap), A) in enumerate(zip(ps_list, A_list)):
                sq = sqp.next()
                P.act(sq[0:M, 0:n], ap, AF.Square, [bk], [sq])
                P.mm(psum_h[0:M, bo2:bo2 + n], A, sq[0:M, 0:n], i == 0, i == len(ps_list) - 1, [sq, cmat], [bk2])
            r = rp.next()
            P.act(r[0:M, 0:n], psum_h[0:M, bo2:bo2 + n], AF.Ln, [bk2, epsb], [r], bias=epsb[0:M, 0:1])
            P.act(r[0:M, 0:n], r[0:M, 0:n], AF.Exp, [r], [r], scale=-0.5)
            return r

        epsb = sb([128, 1], F32, name="epsb")
        P.memset('dve', epsb[:, :], EPS, [epsb])

        for l in range(nlayer):
            last = (l == nlayer - 1)
            Hsrc = hT0 if l == 0 else H
            with contextlib.ExitStack() as pes:
                w_in = sb([128, 8 * 2080], BF16, pes, "w_in")
                w_qb = sb([128, 2 * 576], BF16, pes, "w_qb")
                w_kvb = sb([128, 768], BF16, pes, "w_kvb")
                P.dma('pool', w_in[:, :].rearrange("p (k n) -> p k n", k=8),
                      w_in_d[l].rearrange("(k p) n -> p k n", p=128), [], [w_in])
                P.dma('pool', w_qb[:, :].rearrange("p (k n) -> p k n", k=2),
                      w_qb_d[l].rearrange("(k p) n -> p k n", p=128), [], [w_qb])
                P.dma('pool', w_kvb[:, :], w_kvb_d[l], [], [w_kvb])
                hin_p = sbpool(2, [128, 8 * 512], F32, pes, "hin")
                sqb_p = sbpool(1, [128, 8 * 512], BF16, pes, "sqb")
                xn_p = sbpool(2, [128, 8 * 512], BF16, pes, "xn")
                tmp_p = sbpool(4, [128, 512], F32, pes, "tmp")
                sq_o = Own([sb([128, 512], BF16, pes, "sq") for _ in range(7)])
                r_p = sbpool(4, [128, 512], F32, pes, "r")
                rn_p = sbpool(2, [128, 512], F32, pes, "rn")
                ob_o = Own([sb([128, 512], BF16, pes, "ob") for _ in range(10)])
                of_p = sbpool(4, [128, 512], F32, pes, "of")
                cqn_p = sbpool(2, [128, 2 * 512], BF16, pes, "cqn")
                ckvn_p = sbpool(2, [128, 512], BF16, pes, "ckvn")
                vt_p = sbpool(4, [128, 768], BF16, pes, "vt")
                for vt in vt_p.tiles:
                    P.memset('pool', vt[:, :], 1.0, [vt])
                bo_ = Own(list(range(8)))

                def pbank():
                    i = bo_.get()
                    return i, banks[i], i * 512

                def t_norm(C):
                    b, t0, n, is_ctx, col = C['b'], C['t0'], C['n'], C['is_ctx'], C['col']
                    hin = hin_p.next()
                    P.dma('sp', hin[:, :].rearrange("p (k t) -> p k t", k=8)[:, :, 0:n],
                          Hsrc[b, :, t0:t0 + n].rearrange("(k p) t -> p k t", p=128), [], [hin])
                    sqb = sqb_p.next()
                    P.act(sqb[:, :].rearrange("p (k t) -> p k t", k=8)[:, :, 0:n],
                          hin[:, :].rearrange("p (k t) -> p k t", k=8)[:, :, 0:n], AF.Square, [hin], [sqb])
                    bi, bk, bo = pbank()
                    for k in range(8):
                        P.mm(psum_h[:, bo:bo + n], cm(ONES1024), sqb[:, k * 512:k * 512 + n], k == 0, k == 7,
                             [sqb, cmat], [bk])
                    yield
                    rstd = rn_p.next()
                    P.act(rstd[:, 0:n], psum_h[:, bo:bo + n], AF.Ln, [bk, epsb], [rstd], bias=epsb[:, 0:1])
                    bo_.put(bi)
                    P.act(rstd[:, 0:n], rstd[:, 0:n], AF.Exp, [rstd], [rstd], scale=-0.5)
                    xn = C['xn']
                    for k in range(8):
                        tmp = tmp_p.next()
                        o = l * 24 + k * 3 + col
                        P.stt('dve', tmp[:, 0:n], hin[:, k * 512:k * 512 + n], gsm[:, o:o + 1], rstd[:, 0:n],
                              ALU.mult, ALU.mult, [hin, gsm, rstd], [tmp])
                        P.act(xn[:, k * 512:k * 512 + n], tmp[:, 0:n], AF.Identity, [tmp, mod], [xn],
                              bias=modv(l, 0, k, col))
                    C['ready'] = True

                def proj(C, c0, M):
                    n, xn = C['n'], C['xn']
                    bi, bk_, bo2 = pbank()
                    for k in range(8):
                        P.mm(psum_h[0:M, bo2:bo2 + n], w_in[:, k * 2080 + c0:k * 2080 + c0 + M],
                             xn[:, k * 512:k * 512 + n], k == 0, k == 7, [w_in, xn], [bk_])
                    return bi, bk_, psum_h[0:M, bo2:bo2 + n]

                def square(bk_, pap, M, n):
                    sq = sq_o.get()
                    P.act(sq[0:M, 0:n], pap, AF.Square, [bk_], [sq])
                    return sq

                def stats(sqs, As, M, n):
                    bi, bk2, bo2 = pbank()
                    for i, (sq, A) in enumerate(zip(sqs, As)):
                        P.mm(psum_h[0:M, bo2:bo2 + n], A, sq[0:M, 0:n], i == 0, i == len(sqs) - 1, [sq, cmat], [bk2])
                        sq_o.put(sq)
                    r = r_p.next()
                    P.act(r[0:M, 0:n], psum_h[0:M, bo2:bo2 + n], AF.Ln, [bk2, epsb], [r], bias=epsb[0:M, 0:1])
                    bo_.put(bi)
                    P.act(r[0:M, 0:n], r[0:M, 0:n], AF.Exp, [r], [r], scale=-0.5)
                    return r

                def rope_mm(C, src, M):
                    n = C['n']
                    bi, bk_, bo2 = pbank()
                    P.mm(psum_h[0:M, bo2:bo2 + n], cm(PROPE, M), src[0:M, 0:n], True, True, [src, cmat], [bk_])
                    return bi, bk_, bo2

                def rope_fin(C, src, M, dst, bi, bk_, bo2):
                    n, t0 = C['n'], C['t0']
                    p0 = t0 - TC
                    t1 = tmp_p.next()
                    P.tt('dve', t1[0:M, 0:n], src[0:M, 0:n], rope[0:M, p0:p0 + n], ALU.mult, [src, rope], [t1])
                    t2 = tmp_p.next()
                    P.tt('dve', t2[0:M, 0:n], psum_h[0:M, bo2:bo2 + n], rope[0:M, TL + p0:TL + p0 + n],
                         ALU.mult, [bk_, rope], [t2])
                    bo_.put(bi)
                    P.tt('dve', dst[0:M, 0:n], t1[0:M, 0:n], t2[0:M, 0:n], ALU.add, [t1, t2], [dst])

                def t_na(C, which, j):
                    b, t0, n = C['b'], C['t0'], C['n']
                    cbase, gcol, dst = (0, 64, QNA) if which == "q" else (384, 65, KNA)
                    bi, bkp, pap = proj(C, cbase + j * 128, 128)
                    sq = square(bkp, pap, 128, n)
                    yield
                    r = stats([sq], [cm(A64X2)], 128, n)
                    ob = ob_o.get()
                    P.stt('dve', ob[:, 0:n], pap, sm(l, gcol), r[:, 0:n], ALU.mult, ALU.mult,
                          [bkp, smalls, r], [ob])
                    bo_.put(bi)
                    P.store(dst[b, j * 128:(j + 1) * 128, t0:t0 + n], ob[:, 0:n], [ob], [])
                    ob_o.put(ob)

                def t_v(C, tt_, src_kind):
                    b, t0, n, xn = C['b'], C['t0'], C['n'], C['xn']
                    bi, bk_, bo2 = pbank()
                    if src_kind == "na":
                        for k in range(8):
                            P.mm(psum_h[:, bo2:bo2 + 384], xn[:, k * 512 + tt_ * 128:k * 512 + (tt_ + 1) * 128],
                                 w_in[:, k * 2080 + 768:k * 2080 + 1152], k == 0, k == 7, [w_in, xn], [bk_])
                        dst = VNA
                    else:
                        ckvn = C['ckvn']
                        P.mm(psum_h[:, bo2:bo2 + 384], ckvn[:, tt_ * 128:(tt_ + 1) * 128], w_kvb[:, 384:768],
                             True, True, [w_kvb, ckvn], [bk_])
                        dst = VM
                    vt = vt_p.next()
                    P.copy('dve', vt[:, :].rearrange("p (h e) -> p h e", h=6)[:, :, 0:64],
                           psum_h[:, bo2:bo2 + 384].rearrange("p (h e) -> p h e", h=6), [bk_], [vt])
                    bo_.put(bi)
                    P.store(dst[b, (t0 // 128) + tt_], vt[:, :], [vt], [])
                    return
                    yield

                def t_cq(C):
                    n = C['n']
                    pl = [proj(C, 1152 + j * 128, 128) for j in range(2)]
                    sqs = [square(pl[j][1], pl[j][2], 128, n) for j in range(2)]
                    yield
                    r = stats(sqs, [cm(ONES256), cm(ONES256)], 128, n)
                    cqn = C['cqn']
                    for j in range(2):
                        P.stt('dve', cqn[:, j * 512:j * 512 + n], pl[j][2], sm(l, 66 + j), r[:, 0:n],
                              ALU.mult, ALU.mult, [pl[j][1], smalls, r], [cqn])
                        bo_.put(pl[j][0])

                def t_qhead(C, h):
                    b, t0, n, is_ctx, cqn = C['b'], C['t0'], C['n'], C['is_ctx'], C['cqn']
                    bi, bk_, bo2 = pbank()
                    for j in range(2):
                        P.mm(psum_h[0:96, bo2:bo2 + n], w_qb[:, j * 576 + h * 96:j * 576 + (h + 1) * 96],
                             cqn[:, j * 512:j * 512 + n], j == 0, j == 1, [w_qb, cqn], [bk_])
                    pap = psum_h[0:96, bo2:bo2 + n]
                    sq = square(bk_, pap, 96, n)
                    yield
                    r = stats([sq], [cm(A96, 96)], 96, n)
                    ob = ob_o.get()
                    P.stt('dve', ob[0:96, 0:n], pap, sm(l, 69, 96), r[0:96, 0:n], ALU.mult, ALU.mult,
                          [bk_, smalls, r], [ob])
                    bo_.put(bi)
                    if not is_ctx:
                        yield
                        rb = rope_mm(C, ob, 96)
                        yield
                        ob2 = ob_o.get()
                        rope_fin(C, ob, 96, ob2, *rb)
                        ob_o.put(ob)
                        ob = ob2
                    P.store(QM[b, h, :, t0:t0 + n], ob[0:96, 0:n], [ob], [])
                    ob_o.put(ob)

                def t_ckv(C):
                    n = C['n']
                    bi, bkp, pap = proj(C, 1408, 128)
                    sq = square(bkp, pap, 128, n)
                    yield
                    r = stats([sq], [cm(ONES128)], 128, n)
                    ckvn = C['ckvn']
                    P.stt('dve', ckvn[:, 0:n], pap, sm(l, 68), r[:, 0:n], ALU.mult, ALU.mult,
                          [bkp, smalls, r], [ckvn])
                    bo_.put(bi)

                def t_knope(C, j):
                    b, t0, n, ckvn = C['b'], C['t0'], C['n'], C['ckvn']
                    bi, bk_, bo2 = pbank()
                    P.mm(psum_h[:, bo2:bo2 + n], w_kvb[:, j * 128:(j + 1) * 128], ckvn[:, 0:n], True, True,
                         [w_kvb, ckvn], [bk_])
                    pap = psum_h[:, bo2:bo2 + n]
                    sq = square(bk_, pap, 128, n)
                    yield
                    r = stats([sq], [cm(A64X2)], 128, n)
                    ob = ob_o.get()
                    P.stt('dve', ob[:, 0:n], pap, sm(l, 70), r[:, 0:n], ALU.mult, ALU.mult,
                          [bk_, smalls, r], [ob])
                    bo_.put(bi)
                    P.store(KM[b, 2 * j, 0:64, t0:t0 + n], ob[0:64, 0:n], [ob], [])
                    P.store(KM[b, 2 * j + 1, 0:64, t0:t0 + n], ob[64:128, 0:n], [ob], [])
                    ob_o.put(ob)

                def t_kr(C):
                    b, t0, n, is_ctx = C['b'], C['t0'], C['n'], C['is_ctx']
                    bi, bkp, pap = proj(C, 1472, 96)
                    sq = square(bkp, pap, 96, n)
                    yield
                    r = stats([sq], [cm(AKR, 96)], 96, n)
                    ob = ob_o.get()
                    P.stt('dve', ob[0:96, 0:n], pap, sm(l, 71, 96), r[0:96, 0:n], ALU.mult, ALU.mult,
                          [bkp, smalls, r], [ob])
                    bo_.put(bi)
                    if not is_ctx:
                        yield
                        rb = rope_mm(C, ob, 96)
                        yield
                        ob2 = ob_o.get()
                        rope_fin(C, ob, 96, ob2, *rb)
                        ob_o.put(ob)
                        ob = ob2
                    for h in range(6):
                        P.store(KM[b, h, 64:96, t0:t0 + n], ob[64:96, 0:n], [ob], [])
                    ob_o.put(ob)

                def t_l(C, kind, j):
                    b, t0, n = C['b'], C['t0'], C['n']
                    c0 = (1568 if kind == "x" else 1824) + j * 128
                    bi, bkp, pap = proj(C, c0, 128)
                    of = of_p.next()
                    if kind == "x":
                        P.copy('act', of[:, 0:n], pap, [bkp], [of])
                        P.store(LX[b, j * 128:(j + 1) * 128, t0:t0 + n], of[:, 0:n], [of], [])
                    else:
                        P.act(of[:, 0:n], pap, AF.Gelu_apprx_tanh, [bkp], [of])
                        P.store(GL[b, j * 128:(j + 1) * 128, t0:t0 + n], of[:, 0:n], [of], [])
                    bo_.put(bi)
                    return
                    yield

                Cs = []
                for (b, t0, n, is_ctx) in chunks():
                    Cs.append(dict(b=b, t0=t0, n=n, is_ctx=is_ctx, col=(2 if is_ctx else b), xn=xn_p.next(),
                                   cqn=cqn_p.next(), ckvn=ckvn_p.next()))
                run_tasks([t_norm(Cs[0])], 1)
                tasks = []
                for ci, C in enumerate(Cs):
                    skipq = C['is_ctx'] and last
                    n = C['n']
                    tl = []
                    if not skipq:
                        tl.append(t_cq(C))
                    tl += [t_ckv(C), t_kr(C)]
                    if not skipq:
                        tl += [t_na(C, "q", j) for j in range(3)]
                    tl += [t_na(C, "k", j) for j in range(3)]
                    if ci + 1 < len(Cs):
                        tl.append(NormTask(t_norm(Cs[ci + 1])))
                    mix_a = [t_qhead(C, h) for h in range(6)] if not skipq else []
                    mix_b = [t_knope(C, j) for j in range(3)] + [t_v(C, tt_, "mla") for tt_ in range(n // 128)]
                    while mix_a or mix_b:
                        if mix_a:
                            tl.append(mix_a.pop(0))
                        if mix_b:
                            tl.append(mix_b.pop(0))
                    tl += [t_v(C, tt_, "na") for tt_ in range(n // 128)]
                    tl += [t_l(C, "x", j) for j in range(2)]
                    if not skipq:
                        tl += [t_l(C, "g", j) for j in range(2)]
                    tasks += [t if getattr(t, '_is_norm', False) else (C, t) for t in tl]
                run_tasks(tasks, 4)
            P.barrier("P1")

            with contextlib.ExitStack() as pes:
                qt_p = sbpool(2, [96, T], BF16, pes, "qt")
                kt_p = sbpool(2, [96, T], BF16, pes, "kt")
                v_p = sbpool(2, [128, 18 * 128], BF16, pes, "v")
                tab_p = sbpool(2, [128, 21 * 128], F32, pes, "tab")
                s_p = sbpool(3, [128, 5 * 128], F32, pes, "s")
                pna_p = sbpool(4, [128, 7 * 128], BF16, pes, "pna")
                pm_p = sbpool(4, [128, 1024], BF16, pes, "pm")
                u_p = sbpool(2, [128, T], F32, pes, "u")
                dn_p = sbpool(2, [64, T], F32, pes, "dn")
                o_p = sbpool(2, [64, T], BF16, pes, "o")

                def finish(u, q0, q1, orow, b, dn=None, o=None, eng='act', defer=None):
                    if dn is None:
                        dn = dn_p.next()
                        o = o_p.next()
                    P.store(dn[0:64, q0:q1], u[64:128, q0:q1], [u], [dn])

                    def part2():
                        if eng == 'act':
                            P.act(dn[0:64, q0:q1], dn[0:64, q0:q1], AF.Ln, [dn], [dn])
                            P.act(dn[0:64, q0:q1], dn[0:64, q0:q1], AF.Exp, [dn], [dn], scale=-1.0)
                        else:
                            P.recip(dn[0:64, q0:q1], dn[0:64, q0:q1], [dn], [dn])
                        P.tt('pool', o[0:64, q0:q1], u[0:64, q0:q1], dn[0:64, q0:q1], ALU.mult, [u, dn], [o])
                        P.store(OT[b, orow:orow + 64, q0:q1], o[0:64, q0:q1], [o], [])
                    if defer is None:
                        part2()
                    else:
                        defer.append(part2)

                fin_q = []

                do_ada = (l + 1 < nlayer)
                if do_ada:
                    adap = sbpool(2, [128, 8 * 512], F32, pes, "ada")
                for b in range(NB):
                    for h in range(6):
                        it = b * 6 + h
                        if do_ada:
                            ada_load(l + 1, it, adap.tiles[it % 2])
                            if it > 0:
                                ada_mm(l + 1, it - 1, adap.tiles[(it - 1) % 2])
                        qt = qt_p.next()
                        kt = kt_p.next()
                        v = v_p.next()
                        tab = tab_p.next()
                        P.dma('sp', qt[0:64, :], QNA[b, h * 64:(h + 1) * 64, :], [], [qt])
                        P.dma('sp', kt[0:64, :], KNA[b, h * 64:(h + 1) * 64, :], [], [kt])
                        P.dma('sp', v[:, :].rearrange("p (t e) -> p t e", t=18),
                              VNA[b, :, :, h * 128:(h + 1) * 128].rearrange("t p e -> p t e"), [], [v])
                        P.dma('sp', tab[:, :], natab_d[l, h], [], [tab])
                        u = u_p.next()
                        q_lo = 0 if not last else TC
                        if not last:
                            bks, bos = bank()
                            for kt_i in range(2):
                                P.mm(psum_h[:, bos + kt_i * 256:bos + (kt_i + 1) * 256],
                                     kt[0:64, kt_i * 128:(kt_i + 1) * 128], qt[0:64, 0:TC], True, True,
                                     [kt, qt], [bks])
                            p = pm_p.next()
                            P.act(p[:, 0:512], psum_h[:, bos:bos + 512], AF.Exp, [bks], [p])
                            bko, boo = accbank()
                            for kt_i in range(2):
                                P.mm(psum_h[:, boo:boo + TC], v[:, kt_i * 128:(kt_i + 1) * 128],
                                     p[:, kt_i * 256:(kt_i + 1) * 256], kt_i == 0, kt_i == 1, [v, p], [bko])
                            P.copy('dve', u[:, 0:TC], psum_h[:, boo:boo + TC], [bko], [u])

                        def na_S(i):
                            if i == 0:
                                kts, tb = [3, 2, 1, 0], 0
                            elif i == 1:
                                kts, tb = [3, 2, 1, 0], 4
                            elif i == 14:
                                kts, tb = [15, 14, 13, 12], 13
                            elif i == 15:
                                kts, tb = [15, 14, 13, 12], 17
                            else:
                                kts, tb = [i + 2, i + 1, i, i - 1, i - 2], 8
                            nl = len(kts)
                            q0 = TC + i * 128
                            bkA, bkB, so = bank2()
                            tiles = [2 + x for x in kts] + [0, 1]
                            for idx, kti in enumerate(tiles):
                                P.mm(psum_h[:, so + idx * 128:so + (idx + 1) * 128],
                                     kt[0:64, kti * 128:(kti + 1) * 128], qt[0:64, q0:q0 + 128], True, True,
                                     [kt, qt], [bkA if idx < 4 else bkB])
                            s_ = s_p.next()
                            P.tt('dve', s_[:, 0:nl * 128], psum_h[:, so:so + nl * 128],
                                 tab[:, tb * 128:(tb + nl) * 128], ALU.add, [bkA, bkB, tab], [s_])
                            p = pna_p.next()
                            P.act(p[:, 0:nl * 128], s_[:, 0:nl * 128], AF.Exp, [s_], [p])
                            P.act(p[:, nl * 128:(nl + 2) * 128], psum_h[:, so + nl * 128:so + (nl + 2) * 128],
                                  AF.Exp, [bkA, bkB], [p])
                            return i, p, tiles

                        acc = {}

                        def na_PV(i, p, tiles):
                            if i % 4 == 0:
                                acc['b'] = accbank()
                            bko, boo = acc['b']
                            for idx, kti in enumerate(tiles):
                                P.mm(psum_h[:, boo + (i % 4) * 128:boo + (i % 4 + 1) * 128],
                                     v[:, kti * 128:(kti + 1) * 128], p[:, idx * 128:(idx + 1) * 128],
                                     idx == 0, idx == len(tiles) - 1, [v, p], [bko])
                            if i % 4 == 3:
                                qs = TC + (i - 3) * 128
                                P.copy('dve', u[:, qs:qs + 512], psum_h[:, boo:boo + 512], [bko], [u])
                        pend = []
                        for i in range(16):
                            pend.append(na_S(i))
                            if len(pend) > 2:
                                na_PV(*pend.pop(0))
                            if i == 5:
                                while fin_q:
                                    fin_q.pop(0)()
                        while pend:
                            na_PV(*pend.pop(0))
                        finish(u, q_lo, T, h * 64, b, defer=fin_q)
                while fin_q:
                    fin_q.pop(0)()
                P.mark("NA")
                if do_ada:
                    ada_mm(l + 1, 11, adap.tiles[1])
                    ada_finish(l + 1)
                mfin = []
                for b in range(NB):
                    for h in range(6):
                        qt = qt_p.next()
                        kt = kt_p.next()
                        v = v_p.next()
                        P.dma('sp', qt[0:96, :], QM[b, h], [], [qt])
                        P.dma('sp', kt[0:96, :], KM[b, h], [], [kt])
                        P.dma('sp', v[:, :].rearrange("p (t e) -> p t e", t=18),
                              VM[b, :, :, h * 128:(h + 1) * 128].rearrange("t p e -> p t e"), [], [v])
                        u = u_p.next()
                        qchunks = [(TC + c * 512, 512, 18) for c in range(4)]
                        if not last:
                            qchunks = [(0, TC, 2)] + qchunks
                        items = [(q0, qn, kti, nk) for (q0, qn, nk) in qchunks for kti in range(0, nk, 2)]
                        mdn = dn_p.next()
                        mo_ = o_p.next()

                        def m_S(item):
                            q0, qn, kti, nk = item
                            bkA, bkB, so = bank2()
                            for t_ in range(2):
                                P.mm(psum_h[:, so + t_ * 512:so + t_ * 512 + qn],
                                     kt[0:96, (kti + t_) * 128:(kti + t_ + 1) * 128],
                                     qt[0:96, q0:q0 + qn], True, True, [kt, qt], [bkA if t_ == 0 else bkB])
                            p = pm_p.next()
                            P.act(p[:, :].rearrange("p (t q) -> p t q", t=2)[:, :, 0:qn],
                                  psum_h[:, so:so + 1024].rearrange("p (t q) -> p t q", t=2)[:, :, 0:qn],
                                  AF.Exp, [bkA, bkB], [p])
                            return item, p

                        macc = {}

                        def m_PV(item, p):
                            q0, qn, kti, nk = item
                            if kti == 0:
                                macc['b'] = accbank()
                            bko, boo = macc['b']
                            for t_ in range(2):
                                P.mm(psum_h[:, boo:boo + qn], v[:, (kti + t_) * 128:(kti + t_ + 1) * 128],
                                     p[:, t_ * 512:t_ * 512 + qn],
                                     kti + t_ == 0, kti + t_ == nk - 1, [v, p], [bko])
                            if kti + 2 >= nk:
                                P.copy('dve', u[:, q0:q0 + qn], psum_h[:, boo:boo + qn], [bko], [u])
                                while mfin:
                                    mfin.pop(0)()
                                finish(u, q0, q0 + qn, 384 + h * 64, b, mdn, mo_, 'dve', defer=mfin)
                        pend = []
                        for item in items:
                            pend.append(m_S(item))
                            if len(pend) > 2:
                                m_PV(*pend.pop(0))
                        while pend:
                            m_PV(*pend.pop(0))
                while mfin:
                    mfin.pop(0)()
            P.barrier("ATT")

            with contextlib.ExitStack() as pes:
                wbd = sb([128, 8 * 128], BF16, pes, "wbd")
                P.dma('pool', wbd[:, :], wbd_d[l], [], [wbd])
                sets = []
                for si in range(2):
                    sets.append(dict(
                        lx=sb([128, T], F32, pes, "lx"), xc=sb([128, T], F32, pes, "xc"),
                        xcb=sb([128, T], BF16, pes, "xcb"), a=sb([128, T], F32, pes, "a"),
                        sx=sb([128, T], F32, pes, "sx"), uu=sb([128, T], F32, pes, "lu"),
                        hf=sb([128, T], F32, pes, "hf"), hb=sb([128, T], F32, pes, "hb"),
                        ol=sb([128, T], BF16, pes, "ol")))
                segs = [(0, TC), (TC, T)]

                def t_lru(b, j, S):
                    lx, xc, xcb, a, sx, uu, hf, hb, ol = (S[k] for k in
                                                          ("lx", "xc", "xcb", "a", "sx", "uu", "hf", "hb", "ol"))
                    P.dma('sp', lx[:, :], LX[b, j * 128:(j + 1) * 128, :], [], [lx])
                    cw = lambda tap: sm(l, 72 + j * 4 + tap)
                    for (s0, s1) in segs:
                        P.ts('dve', xc[:, s0:s1], lx[:, s0:s1], cw(1), sm(l, 80 + j), ALU.mult, ALU.add,
                             [lx, smalls], [xc])
                        P.stt('dve', xc[:, s0 + 1:s1], lx[:, s0:s1 - 1], cw(0), xc[:, s0 + 1:s1],
                              ALU.mult, ALU.add, [lx, smalls, xc], [xc])
                        P.stt('dve', xc[:, s0:s1 - 1], lx[:, s0 + 1:s1], cw(2), xc[:, s0:s1 - 1],
                              ALU.mult, ALU.add, [lx, smalls, xc], [xc])
                        P.stt('dve', xc[:, s0:s1 - 2], lx[:, s0 + 2:s1], cw(3), xc[:, s0:s1 - 2],
                              ALU.mult, ALU.add, [lx, smalls, xc], [xc])
                    P.copy('act', xcb[:, :], xc[:, :], [xc], [xcb])
                    gl = lx
                    P.dma('sp', gl[:, :], GL[b, j * 128:(j + 1) * 128, :], [], [gl])
                    yield
                    for d in range(2):
                        for gate, dstt, bcol in ((0, a, 82), (1, sx, 86)):
                            wi = (gate * 2 + d) * 2 + j
                            for (c0, cn) in [(0, 512), (512, 512), (1024, 512), (1536, 512), (2048, 256)]:
                                bk_, bo_ = bank()
                                P.mm(psum_h[:, bo_:bo_ + cn], wbd[:, wi * 128:(wi + 1) * 128],
                                     xcb[:, c0:c0 + cn], True, True, [wbd, xcb], [bk_])
                                P.act(dstt[:, c0:c0 + cn], psum_h[:, bo_:bo_ + cn], AF.Sigmoid,
                                      [bk_, smalls], [dstt], bias=sm(l, bcol + d * 2 + j))
                        nci = l * 4 + d * 2 + j
                        P.act(uu[:, :], a[:, :], AF.Exp, [a, negc2], [uu], scale=negc2[:, nci:nci + 1])
                        P.act(a[:, :], a[:, :], AF.Exp, [a, negc], [a], scale=negc[:, nci:nci + 1])
                        P.act(uu[:, :], uu[:, :], AF.Sqrt, [uu], [uu], bias=1.0, scale=-1.0)
                        yield
                        P.tt('dve', uu[:, :], uu[:, :], sx[:, :], ALU.mult, [uu, sx], [uu])
                        P.tt('dve', uu[:, :], uu[:, :], xc[:, :], ALU.mult, [uu, xc], [uu])
                        if d == 0:
                            P.op('dve', lambda g, hf=hf, a=a, uu=uu: g.tensor_tensor_scan(
                                out=hf[:, :], data0=a[:, :], data1=uu[:, :], initial=0.0,
                                op0=ALU.mult, op1=ALU.add), [a, uu], [hf])
                        else:
                            P.op('dve', lambda g, hb=hb, a=a, uu=uu: g.tensor_tensor_scan(
                                out=hb[:, 0:TC], data0=a[:, 0:TC][:, ::-1], data1=uu[:, 0:TC][:, ::-1],
                                initial=0.0, op0=ALU.mult, op1=ALU.add), [a, uu], [hb])
                            P.op('dve', lambda g, hb=hb, a=a, uu=uu: g.tensor_tensor_scan(
                                out=hb[:, TC:T], data0=a[:, TC:T][:, ::-1], data1=uu[:, TC:T][:, ::-1],
                                initial=hb[:, TC - 1:TC], op0=ALU.mult, op1=ALU.add), [a, uu, hb], [hb])
                        yield
                    for (s0, s1) in segs:
                        P.tt('dve', hf[:, s0:s1], hf[:, s0:s1], hb[:, s0:s1][:, ::-1], ALU.add, [hf, hb], [hf])
                    P.tt('pool', ol[:, :], hf[:, :], gl[:, :], ALU.mult, [hf, gl], [ol])
                    q_lo = TC if last else 0
                    P.store(OT[b, 768 + j * 128:768 + (j + 1) * 128, q_lo:T], ol[:, q_lo:T], [ol], [])

                run_tasks([t_lru(b, j, sets[(b * 2 + j) % 2]) for b in range(NB) for j in range(2)], 2)
            P.barrier("LRU")

            with contextlib.ExitStack() as pes:
                wo = sb([128, 8 * D], BF16, pes, "wo")
                P.dma('pool', wo[:, :].rearrange("p (k n) -> p k n", k=8),
                      w_out_d[l].rearrange("(k p) n -> p k n", p=128), [], [wo])
                wgs = [sb([128, 8 * 256], BF16, pes, "wg") for _ in range(6)]
                wus = [sb([128, 8 * 256], BF16, pes, "wu") for _ in range(6)]
                wds = [sb([128, 2 * D], BF16, pes, "wd") for _ in range(6)]
                N2 = 512
                hin_p = sbpool(2, [128, 8 * N2], F32, pes, "h2")
                ot_p = sbpool(1, [128, 8 * N2], BF16, pes, "ot")
                sqb_p = sbpool(1, [128, 8 * N2], BF16, pes, "sqb2")
                xn_p = sbpool(2, [128, 8 * N2], BF16, pes, "xn2")
                r_p = sbpool(2, [128, N2], F32, pes, "r2")
                tmp_p = sbpool(3, [128, N2], F32, pes, "tmp2")
                sl_p = sbpool(3, [128, N2], F32, pes, "sl")
                g_p = sbpool(1, [128, 12 * N2], BF16, pes, "g")
                hkeys = {}

                def v3(t, n):
                    return t[:, :].rearrange("p (k t) -> p k t", k=8)[:, :, 0:n]

                def d3(dt, b, t0, n):
                    return dt[b, :, t0:t0 + n].rearrange("(k p) t -> p k t", p=128)
                for half in range(2):
                    sl_ids = list(range(6)) if half == 0 else list(range(6, 11))
                    for si, m2 in enumerate(sl_ids):
                        P.dma('pool', wgs[si][:, :].rearrange("p (k n) -> p k n", k=8),
                              wg_d[l, :, m2 * 256:(m2 + 1) * 256].rearrange("(k p) n -> p k n", p=128), [], [wgs[si]])
                        P.dma('pool', wus[si][:, :].rearrange("p (k n) -> p k n", k=8),
                              wu_d[l, :, m2 * 256:(m2 + 1) * 256].rearrange("(k p) n -> p k n", p=128), [], [wus[si]])
                    for si, m2 in enumerate(sl_ids):
                        P.dma('pool', wds[si][:, :].rearrange("p (k n) -> p k n", k=2),
                              wd_d[l, m2 * 256:(m2 + 1) * 256, :].rearrange("(k p) n -> p k n", p=128), [], [wds[si]])
                    nm = 2 * len(sl_ids)
                    def bank4():
                        i = b4[0]
                        b4[0] = (b4[0] + 1) % 4
                        return banks[i], i * 512

                    def t_ef(half, nm, ci, b, t0, n, is_ctx, hin, xn):
                        col = 2 if is_ctx else b
                        hk = hkeys.setdefault((b, t0), Tl(None, f"Hk{l}_{b}_{t0}"))
                        xk = hkeys.setdefault((b, t0, 'x'), Tl(None, f"Xk{l}_{b}_{t0}"))
                        if half == 0:
                            P.dma('sp', v3(hin, n), d3(Hsrc, b, t0, n), [], [hin])
                            ot = ot_p.next()
                            P.dma('sp', v3(ot, n), d3(OT, b, t0, n), [], [ot])
                            for m in range(8):
                                bk_, bo_ = bank4()
                                for k in range(8):
                                    P.mm(psum_h[:, bo_:bo_ + n], wo[:, k * D + m * 128:k * D + (m + 1) * 128],
                                         ot[:, k * N2:k * N2 + n], k == 0, k == 7, [wo, ot], [bk_])
                                P.stt('dve', hin[:, m * N2:m * N2 + n], psum_h[:, bo_:bo_ + n],
                                      modv(l, 2, m, col), hin[:, m * N2:m * N2 + n], ALU.mult, ALU.add,
                                      [bk_, mod, hin], [hin])
                            sqb = sqb_p.next()
                            P.act(v3(sqb, n), v3(hin, n), AF.Square, [hin], [sqb])
                            bk, bo = banks[4 + ci % 2], (4 + ci % 2) * 512
                            for k in range(8):
                                P.mm(psum_h[:, bo:bo + n], cm(ONES1024), sqb[:, k * N2:k * N2 + n], k == 0,
                                     k == 7, [sqb, cmat], [bk])
                            yield
                            rstd = r_p.next()
                            P.act(rstd[:, 0:n], psum_h[:, bo:bo + n], AF.Ln, [bk, epsb], [rstd],
                                  bias=epsb[:, 0:1])
                            P.act(rstd[:, 0:n], rstd[:, 0:n], AF.Exp, [rstd], [rstd], scale=-0.5)
                            for k in range(8):
                                tmp = tmp_p.next()
                                o = l * 24 + k * 3 + col
                                P.stt('dve', tmp[:, 0:n], hin[:, k * N2:k * N2 + n], gsf[:, o:o + 1],
                                      rstd[:, 0:n], ALU.mult, ALU.mult, [hin, gsf, rstd], [tmp])
                                P.act(xn[:, k * N2:k * N2 + n], tmp[:, 0:n], AF.Identity, [tmp, mod], [xn],
                                      bias=modv(l, 3, k, col))
                            P.store(d3(XN, b, t0, n), v3(xn, n), [xn], [xk])
                            yield
                        else:
                            P.dma('sp', v3(hin, n), d3(H2, b, t0, n), [hk], [hin])
                            P.dma('sp', v3(xn, n), d3(XN, b, t0, n), [xk], [xn])
                            yield
                        g = g_p.next()
                        for m in range(nm):
                            wgt, wut = wgs[m // 2], wus[m // 2]
                            bkg, bog = bank4()
                            bku, bou = bank4()
                            for k in range(8):
                                P.mm(psum_h[:, bog:bog + n],
                                     wgt[:, k * 256 + (m % 2) * 128:k * 256 + (m % 2 + 1) * 128],
                                     xn[:, k * N2:k * N2 + n], k == 0, k == 7, [wgt, xn], [bkg])
                            for k in range(8):
                                P.mm(psum_h[:, bou:bou + n],
                                     wut[:, k * 256 + (m % 2) * 128:k * 256 + (m % 2 + 1) * 128],
                                     xn[:, k * N2:k * N2 + n], k == 0, k == 7, [wut, xn], [bku])
                            sl = sl_p.next()
                            P.act(sl[:, 0:n], psum_h[:, bog:bog + n], AF.Silu, [bkg], [sl])
                            P.tt('dve', g[:, m * N2:m * N2 + n], sl[:, 0:n], psum_h[:, bou:bou + n], ALU.mult,
                                 [sl, bku], [g])
                        for mo in range(8):
                            bk_, bo_ = accbank()
                            for m in range(nm):
                                wdt = wds[m // 2]
                                P.mm(psum_h[:, bo_:bo_ + n],
                                     wdt[:, (m % 2) * D + mo * 128:(m % 2) * D + (mo + 1) * 128],
                                     g[:, m * N2:m * N2 + n], m == 0, m == nm - 1, [wdt, g], [bk_])
                            P.stt('dve', hin[:, mo * N2:mo * N2 + n], psum_h[:, bo_:bo_ + n],
                                  modv(l, 5, mo, col), hin[:, mo * N2:mo * N2 + n], ALU.mult, ALU.add,
                                  [bk_, mod, hin], [hin])
                        if half == 0:
                            P.store(d3(H2, b, t0, n), v3(hin, n), [hin], [hk])
                        elif last:
                            P.store(outT[b, :, t0 - TC:t0 - TC + n].rearrange("(k p) t -> p k t", p=128),
                                    v3(hin, n), [hin], [])
                        else:
                            P.store(d3(H, b, t0, n), v3(hin, n), [hin], [])

                    b4 = [0]
                    efl = []
                    ci = 0
                    for (b, t0, n, is_ctx) in chunks():
                        if is_ctx and last:
                            continue
                        efl.append(t_ef(half, nm, ci, b, t0, n, is_ctx, hin_p.tiles[ci % 2], xn_p.tiles[ci % 2]))
                        ci += 1
                    run_tasks(efl, 2)
            P.barrier("EF")
        P.emit(es)
    return nc


def _const_mats():
    m = np.zeros((7, 128, 128), np.float32)
    m[0] = 1.0 / 1024
    m[1] = 1.0 / 256
    m[2] = 1.0 / 128
    m[3, :64, :64] = 1.0 / 64
    m[3, 64:, 64:] = 1.0 / 64
    m[4, :64, :64] = 1.0 / 64
    m[4, 64:96, 64:96] = 1.0 / 32
    m[5, :64, :64] = 1.0 / 64
    m[5, 64:96, 64:96] = 1.0 / 32
    for base in (64, 80):
        for i in range(8):
            m[6, base + 8 + i, base + i] = -1.0
            m[6, base + i, base + 8 + i] = 1.0
    return np.ascontiguousarray(m.transpose(1, 0, 2).reshape(128, 7 * 128))


def _rope_tables():
    t = np.arange(TL)
    row = (t // 64).astype(np.float32)
    colp = (t % 64).astype(np.float32)
    inv = (np.float32(10000.0) ** (-np.arange(8, dtype=np.float32) / np.float32(8))).astype(np.float32)
    ar = row[:, None] * inv
    ac = colp[:, None] * inv
    cos = np.ones((96, TL), np.float32)
    sin = np.zeros((96, TL), np.float32)
    for base, ang in ((64, ar), (80, ac)):
        cos[base:base + 8] = np.cos(ang).T
        cos[base + 8:base + 16] = np.cos(ang).T
        sin[base:base + 8] = np.sin(ang).T
        sin[base + 8:base + 16] = np.sin(ang).T
    return np.ascontiguousarray(np.concatenate([cos, sin], axis=1))


def _na_index():
    idx = np.full((21, 128, 128), -1, np.int64)
    specs = [(0, [3, 2, 1, 0]), (1, [3, 2, 1, 0]), (5, [7, 6, 5, 4, 3]), (14, [15, 14, 13, 12]),
             (15, [15, 14, 13, 12])]
    kc = np.arange(64)[:, None]
    qc = np.arange(64)[None, :]
    qcs = np.clip(qc - 8, 0, 48)
    col_ok = (kc >= qcs) & (kc < qcs + 16)
    cr = np.clip(kc - qc, -15, 15) + 15
    ti = 0
    for i, kts in specs:
        for kt in kts:
            for a in range(2):
                for b in range(2):
                    kr = 2 * kt + a
                    qr = 2 * i + b
                    rs = min(max(qr - 4, 0), 24)
                    if rs <= kr < rs + 8:
                        dr = kr - qr + 7
                        blk = np.where(col_ok, dr * 31 + cr, -1)
                        idx[ti, a * 64:(a + 1) * 64, b * 64:(b + 1) * 64] = blk
            ti += 1
    return idx


def _prep_shared(inp):
    f = lambda k: np.asarray(inp[k], np.float32)
    sh = {}
    sh["ada_w"] = f("ada_w")
    sm = np.zeros((128, NLAYER, SL), np.float32)
    ada_b = f("ada_b")
    p = np.arange(128)
    for l in range(NLAYER):
        sm[:, l, 0:48] = ada_b[l].reshape(48, 128).T
        sm[:, l, 48:56] = f("norm_mix_g")[l].reshape(8, 128).T
        sm[:, l, 56:64] = f("norm_ffn_g")[l].reshape(8, 128).T
        sm[:, l, 64] = np.tile(f("na_q_g")[l], 2)
        sm[:, l, 65] = np.tile(f("na_k_g")[l], 2)
        sm[:, l, 66:68] = f("mla_cq_g")[l].reshape(2, 128).T
        sm[:, l, 68] = f("mla_ckv_g")[l]
        sm[:96, l, 69] = f("mla_q_g")[l]
        sm[:, l, 70] = np.tile(f("mla_k_g")[l][:64], 2)
        sm[64:96, l, 71] = f("mla_k_g")[l][64:96]
        cw = f("lru_conv_w")[l]
        for j in range(2):
            for tap in range(4):
                sm[:, l, 72 + j * 4 + tap] = cw[tap, j * 128:(j + 1) * 128]
            sm[:, l, 80 + j] = f("lru_conv_b")[l][j * 128:(j + 1) * 128]
            for d in range(2):
                sm[:, l, 82 + d * 2 + j] = f("lru_b_a")[l, d, j * 128:(j + 1) * 128]
                sm[:, l, 86 + d * 2 + j] = f("lru_b_x")[l, d, j * 128:(j + 1) * 128]
                sm[:, l, 90 + d * 2 + j] = f("lru_lambda")[l, d, j * 128:(j + 1) * 128]
    sh["smalls"] = np.ascontiguousarray(sm.reshape(128, NLAYER * SL))
    sh["w_in"] = f("w_in")
    sh["w_qb"] = f("mla_w_qb")
    kvb = f("mla_w_kvb").reshape(NLAYER, 128, 6, 128)
    sh["w_kvb"] = np.ascontiguousarray(
        np.concatenate([kvb[..., :64].reshape(NLAYER, 128, 384), kvb[..., 64:].reshape(NLAYER, 128, 384)], axis=-1))
    wbd = np.zeros((NLAYER, 128, 8, 128), np.float32)
    for gate, key in ((0, "lru_w_a"), (1, "lru_w_x")):
        w = f(key)
        for d in range(2):
            for j in range(2):
                wi = (gate * 2 + d) * 2 + j
                for hh in range(2):
                    wbd[:, hh * 64:(hh + 1) * 64, wi, hh * 64:(hh + 1) * 64] = w[:, d, j * 2 + hh]
    sh["wbd"] = np.ascontiguousarray(wbd.reshape(NLAYER, 128, 8 * 128))
    sh["w_out"] = f("w_out")
    sh["wg"] = f("ffn_w_gate")
    sh["wu"] = f("ffn_w_up")
    sh["wd"] = f("ffn_w_down")
    sh["cmat"] = _const_mats()
    sh["rope"] = _rope_tables()
    idx = _na_index()
    rpb = f("na_rpb").reshape(NLAYER, 6, 15 * 31)
    ext = np.concatenate([rpb, np.full((NLAYER, 6, 1), NEG, np.float32)], axis=-1)
    tab = ext[:, :, idx]
    sh["natab"] = np.ascontiguousarray(tab.transpose(0, 1, 3, 2, 4).reshape(NLAYER, 6, 128, 21 * 128))
    return sh


def _prep_core(inp, core):
    x = np.asarray(inp["x"], np.float32)
    ctx = np.asarray(inp["ctx"], np.float32)
    c = np.asarray(inp["c"], np.float32)
    c_ctx = np.asarray(inp["c_ctx"], np.float32)
    bs = slice(core * NB, (core + 1) * NB)
    hT0 = np.ascontiguousarray(np.concatenate([ctx[bs], x[bs]], axis=1).transpose(0, 2, 1))
    cols = np.stack([c[core * NB], c[core * NB + 1], c_ctx], axis=-1)
    csT = np.ascontiguousarray(cols.reshape(8, 128, 3).transpose(1, 0, 2).reshape(128, 24))
    return {"hT0": hT0, "csT": csT}


_NC_CACHE = {}


def kernel(**inputs):
    ncores = 8
    if "nc" not in _NC_CACHE:
        _NC_CACHE["nc"] = build_program()
    nc = _NC_CACHE["nc"]
    shared = _prep_shared(inputs)
    in_maps = []
    for core in range(ncores):
        m = dict(shared)
        m.update(_prep_core(inputs, core))
        in_maps.append(m)
    res = run_bass_kernel_spmd(nc, in_maps, core_ids=list(range(ncores)))
    outs = [np.asarray(r["outT"]) for r in res.results]
    out = np.concatenate(outs, axis=0).transpose(0, 2, 1)
    return np.ascontiguousarray(out.astype(np.float32))
```
